# Optimizing a Trainium2 kernel written in Bass

```python
import jax, jax.numpy as jnp
from jax import lax
import numpy as np

D_MODEL = 2048
BATCH = 8
SEQ = 4096
DEPTH = 4

N_MIXERS = 2
N_SSD_LAYERS = (DEPTH + N_MIXERS - 1) // N_MIXERS
N_SC_LAYERS = DEPTH // N_MIXERS

SSM_EXPAND = 2
D_INNER = SSM_EXPAND * D_MODEL
SSM_HEAD_DIM = 64
SSM_HEADS = D_INNER // SSM_HEAD_DIM
SSM_GROUPS = 8
SSM_HEADS_PER_GROUP = SSM_HEADS // SSM_GROUPS
SSM_STATE = 128
SSM_CONV = 4
SSM_CHUNK = 128
SSM_BC_DIM = SSM_GROUPS * SSM_STATE
SSM_CONV_DIM = D_INNER + 2 * SSM_BC_DIM
SSM_IN_DIM = D_INNER + SSM_CONV_DIM + SSM_HEADS

SC_DIM = D_MODEL
SC_WIDTH = 3

D_FF = 5632
FFN_CONV = 3

EPS = 1e-5
DT_MIN = 1e-3
DT_MAX = 1e-1
A_MIN = 1.0
A_MAX = 16.0

kernel_name = "hybrid_ssd_shortconv_convffn_trunk"


def rms_norm(x, w):
    x32 = x.astype(jnp.float32)
    y = x32 * lax.rsqrt(jnp.mean(x32 * x32, axis=-1, keepdims=True) + EPS)
    return (y * w.astype(jnp.float32)).astype(x.dtype)


def causal_dwconv(x, w, b=None):
    width, ch = w.shape
    y = lax.conv_general_dilated(
        x, w[:, None, :].astype(x.dtype), window_strides=(1,),
        padding=[(width - 1, 0)], dimension_numbers=("NWC", "WIO", "NWC"),
        feature_group_count=ch)
    if b is not None:
        y = y + b.astype(x.dtype)
    return y


def segsum_exp(a):
    t = a.shape[-1]
    cs = jnp.cumsum(a, axis=-1)
    diff = cs[..., :, None] - cs[..., None, :]
    mask = jnp.tril(jnp.ones((t, t), dtype=bool))
    return jnp.exp(jnp.where(mask, diff, -jnp.inf))


def ssd_chunked(xh, dt, a, bm, cm):
    f32 = jnp.float32
    b_, s_, h_, p_ = xh.shape
    l_ = SSM_CHUNK
    c_ = s_ // l_
    g_, k_, n_ = SSM_GROUPS, SSM_HEADS_PER_GROUP, SSM_STATE
    xdt = (xh.astype(f32) * dt[..., None]).reshape(b_, c_, l_, g_, k_, p_)
    da = jnp.moveaxis((dt * a).reshape(b_, c_, l_, g_, k_), 2, -1)
    a_cs = jnp.cumsum(da, axis=-1)
    bc = bm.astype(f32).reshape(b_, c_, l_, g_, n_)
    cc = cm.astype(f32).reshape(b_, c_, l_, g_, n_)
    cb = jnp.einsum("bclgn,bcsgn->bcgls", cc, bc)
    decay = segsum_exp(da)
    y_diag = jnp.einsum("bcgls,bcgkls,bcsgkp->bclgkp", cb, decay, xdt)
    decay_to_end = jnp.exp(a_cs[..., -1:] - a_cs)
    states = jnp.einsum("bclgn,bcgkl,bclgkp->bcgkpn", bc, decay_to_end, xdt)
    chunk_decay = jnp.exp(a_cs[..., -1])

    def step(hstate, inp):
        st, dec = inp
        return hstate * dec[..., None, None] + st, hstate

    h0 = jnp.zeros((b_, g_, k_, p_, n_), f32)
    _, prev = lax.scan(step, h0, (jnp.moveaxis(states, 1, 0), jnp.moveaxis(chunk_decay, 1, 0)))
    prev = jnp.moveaxis(prev, 0, 1)
    y_off = jnp.einsum("bclgn,bcgkpn,bcgkl->bclgkp", cc, prev, jnp.exp(a_cs))
    return (y_diag + y_off).reshape(b_, s_, h_, p_)


def ssd_mixer(u, w_in, conv_w, conv_b, dt_bias, a_log, d_skip, norm_w, w_out):
    f32 = jnp.float32
    b_, s_, _ = u.shape
    zxbcdt = u @ w_in
    z, xbc, dt_raw = jnp.split(zxbcdt, [D_INNER, D_INNER + SSM_CONV_DIM], axis=-1)
    xbc = jax.nn.silu(causal_dwconv(xbc, conv_w, conv_b))
    xs, bm, cm = jnp.split(xbc, [D_INNER, D_INNER + SSM_BC_DIM], axis=-1)
    dt = jax.nn.softplus(dt_raw.astype(f32) + dt_bias.astype(f32))
    a = -jnp.exp(a_log.astype(f32))
    xh = xs.reshape(b_, s_, SSM_HEADS, SSM_HEAD_DIM)
    y = ssd_chunked(xh, dt, a,
                    bm.reshape(b_, s_, SSM_GROUPS, SSM_STATE),
                    cm.reshape(b_, s_, SSM_GROUPS, SSM_STATE))
    y = y + xh.astype(f32) * d_skip.astype(f32)[:, None]
    yg = (y.reshape(b_, s_, D_INNER) * jax.nn.silu(z.astype(f32)))
    yg = yg.reshape(b_, s_, SSM_GROUPS, D_INNER // SSM_GROUPS)
    yg = yg * lax.rsqrt(jnp.mean(yg * yg, axis=-1, keepdims=True) + EPS)
    y = (yg.reshape(b_, s_, D_INNER) * norm_w.astype(f32)).astype(u.dtype)
    return y @ w_out


def short_conv_mixer(u, w_in, conv_w, w_out):
    bg, cg, h = jnp.split(u @ w_in, 3, axis=-1)
    return (bg * causal_dwconv(cg * h, conv_w)) @ w_out


def conv_ffn(u, w_up, conv_w, conv_b, w_down):
    hu = causal_dwconv(u @ w_up, conv_w, conv_b)
    g, v = jnp.split(hu, 2, axis=-1)
    return (jax.nn.silu(g) * v) @ w_down


def setup_inputs(seed: int = 0) -> dict:
    key = jax.random.key(seed)
    ks = jax.random.split(key, 24)
    f32 = jnp.float32
    nrm = lambda k, shape, scale: jax.random.normal(k, shape, f32) * scale
    La, Lb = N_SSD_LAYERS, N_SC_LAYERS
    dt0 = jnp.exp(jax.random.uniform(ks[5], (La, SSM_HEADS), f32,
                                     float(np.log(DT_MIN)), float(np.log(DT_MAX))))
    return {
        "x": jax.random.normal(ks[0], (BATCH, SEQ, D_MODEL), f32),
        "mix_norm_w": 1.0 + nrm(ks[1], (DEPTH, D_MODEL), 0.02),
        "ffn_norm_w": 1.0 + nrm(ks[2], (DEPTH, D_MODEL), 0.02),
        "final_norm_w": 1.0 + nrm(ks[3], (D_MODEL,), 0.02),
        "ssd_w_in": nrm(ks[4], (La, D_MODEL, SSM_IN_DIM), D_MODEL ** -0.5),
        "ssd_conv_w": nrm(ks[6], (La, SSM_CONV, SSM_CONV_DIM), SSM_CONV ** -0.5),
        "ssd_conv_b": nrm(ks[7], (La, SSM_CONV_DIM), 0.02),
        "ssd_dt_bias": dt0 + jnp.log(-jnp.expm1(-dt0)),
        "ssd_a_log": jnp.log(jax.random.uniform(ks[8], (La, SSM_HEADS), f32, A_MIN, A_MAX)),
        "ssd_d": 1.0 + nrm(ks[9], (La, SSM_HEADS), 0.02),
        "ssd_norm_w": 1.0 + nrm(ks[10], (La, D_INNER), 0.02),
        "ssd_w_out": nrm(ks[11], (La, D_INNER, D_MODEL), D_INNER ** -0.5),
        "sc_w_in": nrm(ks[12], (Lb, D_MODEL, 3 * SC_DIM), D_MODEL ** -0.5),
        "sc_conv_w": nrm(ks[13], (Lb, SC_WIDTH, SC_DIM), SC_WIDTH ** -0.5),
        "sc_w_out": nrm(ks[14], (Lb, SC_DIM, D_MODEL), SC_DIM ** -0.5),
        "ffn_w_up": nrm(ks[15], (DEPTH, D_MODEL, 2 * D_FF), D_MODEL ** -0.5),
        "ffn_conv_w": nrm(ks[16], (DEPTH, FFN_CONV, 2 * D_FF), FFN_CONV ** -0.5),
        "ffn_conv_b": nrm(ks[17], (DEPTH, 2 * D_FF), 0.02),
        "ffn_w_down": nrm(ks[18], (DEPTH, D_FF, D_MODEL), D_FF ** -0.5),
    }


def reference(x, mix_norm_w, ffn_norm_w, final_norm_w,
              ssd_w_in, ssd_conv_w, ssd_conv_b, ssd_dt_bias, ssd_a_log, ssd_d,
              ssd_norm_w, ssd_w_out,
              sc_w_in, sc_conv_w, sc_w_out,
              ffn_w_up, ffn_conv_w, ffn_conv_b, ffn_w_down):
    for i in range(DEPTH):
        j = i // N_MIXERS
        h = rms_norm(x, mix_norm_w[i])
        if i % N_MIXERS == 0:
            x = x + ssd_mixer(h, ssd_w_in[j], ssd_conv_w[j], ssd_conv_b[j], ssd_dt_bias[j],
                              ssd_a_log[j], ssd_d[j], ssd_norm_w[j], ssd_w_out[j])
        else:
            x = x + short_conv_mixer(h, sc_w_in[j], sc_conv_w[j], sc_w_out[j])
        h = rms_norm(x, ffn_norm_w[i])
        x = x + conv_ffn(h, ffn_w_up[i], ffn_conv_w[i], ffn_conv_b[i], ffn_w_down[i])
    return rms_norm(x, final_norm_w)
```

```python
import contextlib
import numpy as np
import concourse.bass as bass
import concourse.mybir as mybir
from concourse.bass_utils import run_bass_kernel_spmd

F32 = mybir.dt.float32
BF16 = mybir.dt.bfloat16
AF = mybir.ActivationFunctionType
ALU = mybir.AluOpType

P = 128
D = 2048
KC = D // P
T = 512
DEPTH = 4
DFF = 5632
FJ = DFF // P
DIN = 4096
NG = 8
HPG = 8
NH = 64
HD = 64
NST = 128
EPS = 1e-5
GBMAX = 32
NSLOT = 3
import os
FORCE_INC = os.environ.get("FORCE_INC") == "1"


def pvec_layout():
    off = {}
    o = 0

    def add(name, n):
        nonlocal o
        off[name] = (o, n)
        o += n
    for l in range(DEPTH):
        add(f"mixnorm{l}", KC)
        add(f"ffnnorm{l}", KC)
        add(f"ffnconv{l}", 2 * FJ * 4)
    add("finalnorm", KC)
    for j in range(2):
        add(f"ssdconv{j}", 48 * 5)
        add(f"ssdnormw{j}", 32)
        add(f"ssdd{j}", 32)
        add(f"dtbias{j}", NH)
        add(f"alog{j}", NH)
        add(f"scconv{j}", KC * 3)
    return off, o


def ssd_chunk_channels(ci):
    g, r = divmod(ci, 6)
    if r < 4:
        return (4 * g + r) * P
    if r == 4:
        return DIN + g * P
    return DIN + NG * NST + g * P


def pack_pvec(inp):
    off, n = pvec_layout()
    pv = np.zeros((P, n), np.float32)

    def put(name, arr):
        o, k = off[name]
        assert arr.shape == (P, k), (name, arr.shape, k)
        pv[:, o:o + k] = arr
    fm = lambda v: np.ascontiguousarray(v.reshape(-1, P).T)
    for l in range(DEPTH):
        put(f"mixnorm{l}", fm(inp["mix_norm_w"][l]))
        put(f"ffnnorm{l}", fm(inp["ffn_norm_w"][l]))
        w = inp["ffn_conv_w"][l]
        b = inp["ffn_conv_b"][l]
        a = np.stack([fm(w[0]), fm(w[1]), fm(w[2]), fm(b)], axis=2)
        put(f"ffnconv{l}", a.reshape(P, -1))
    put("finalnorm", fm(inp["final_norm_w"]))
    for j in range(2):
        w = inp["ssd_conv_w"][j]
        b = inp["ssd_conv_b"][j]
        a = np.zeros((P, 48, 5), np.float32)
        for ci in range(48):
            c0 = ssd_chunk_channels(ci)
            a[:, ci, 0:4] = w[:, c0:c0 + P].T
            a[:, ci, 4] = b[c0:c0 + P]
        put(f"ssdconv{j}", a.reshape(P, -1))
        put(f"ssdnormw{j}", fm(inp["ssd_norm_w"][j]))
        put(f"ssdd{j}", fm(np.repeat(inp["ssd_d"][j], HD)))
        put(f"dtbias{j}", np.broadcast_to(inp["ssd_dt_bias"][j][None, :], (P, NH)))
        put(f"alog{j}", np.broadcast_to(inp["ssd_a_log"][j][None, :], (P, NH)))
        w = inp["sc_conv_w"][j]
        a = np.stack([fm(w[0]), fm(w[1]), fm(w[2])], axis=2)
        put(f"scconv{j}", a.reshape(P, -1))
    return pv


def _fm_group(W, col_starts):
    kc = W.shape[0] // P
    parts = []
    for cs in col_starts:
        a = W[:, cs:cs + P].reshape(kc, P, P).transpose(1, 0, 2)
        parts.append(a)
    return np.stack(parts, axis=1).reshape(P, -1)


def _tm_group(W, c0, ncols):
    kc = W.shape[0] // P
    return W[:, c0:c0 + ncols].reshape(kc, P, ncols).transpose(1, 0, 2).reshape(P, -1)


def mixer_groups(kind):
    raise NotImplementedError


def pack_ffn(w_up, w_down):
    units = []
    for j in range(FJ):
        units.append(_fm_group(w_up, [j * P]))
        units.append(_fm_group(w_up, [DFF + j * P]))
    hk = (FJ // 2) * P
    for m in range(KC):
        units.append(_fm_group(w_down[0:hk], [m * P]))
        units.append(_fm_group(w_down[hk:], [m * P]))
    return units


def pack_sc(w_in, w_out):
    units = []
    for j in range(KC):
        for which in range(3):
            units.append(_fm_group(w_in, [which * D + j * P]))
    for m in range(KC):
        units.append(_fm_group(w_out, [m * P]))
    return units


def pack_ssd(w_in, w_out):
    units = []
    units.append(_tm_group(w_in, 2 * DIN + 2 * NG * NST, NH))
    for g in range(NG):
        xb = DIN
        for i in range(4):
            units.append(_fm_group(w_in, [xb + (4 * g + i) * P]))
        units.append(_fm_group(w_in, [2 * DIN + g * P]))
        units.append(_fm_group(w_in, [2 * DIN + NG * NST + g * P]))
        for q in range(4):
            units.append(_tm_group(w_in[q * 512:(q + 1) * 512], g * 512, 512))
    for m in range(KC):
        units.append(_fm_group(w_out, [m * P]))
    return units


def unit_cols(kind):
    if kind == "ffn":
        return [KC * P] * (2 * FJ) + [(FJ // 2) * P] * (2 * KC)
    if kind == "sc":
        return [KC * P] * (3 * KC) + [KC * P] * KC
    if kind == "ssd":
        s = [KC * NH]
        for g in range(NG):
            s += [KC * P] * 6 + [4 * 512] * 4
        s += [32 * P] * KC
        return s
    raise ValueError(kind)


def group_plan(cols):
    gsz = []
    loc = []
    cur = 0
    for c in cols:
        if gsz and cur + c <= GBMAX * P:
            loc.append((len(gsz) - 1, cur))
            cur += c
            gsz[-1] = cur
        else:
            gsz.append(c)
            loc.append((len(gsz) - 1, 0))
            cur = c
    return gsz, loc


def flat_groups(units, kind):
    cols = [u.shape[1] for u in units]
    assert cols == unit_cols(kind), (kind, cols[:6], unit_cols(kind)[:6])
    gsz, loc = group_plan(cols)
    groups = [[] for _ in gsz]
    for u, (gi, _) in zip(units, loc):
        groups[gi].append(u)
    flat = np.concatenate([np.ascontiguousarray(np.concatenate(g, axis=1)).reshape(-1) for g in groups]).astype(np.float32)
    return flat


def group_sizes(kind):
    return group_plan(unit_cols(kind))[0]


class _Eng:
    def __init__(self, name):
        self.name = name
        self.count = 0
        self.seen = {}
        self.prog = []


class Sched:
    ENGS = ("tensor", "vector", "scalar", "gpsimd", "sync")

    def __init__(self):
        self.E = {n: _Eng(n) for n in self.ENGS}
        self.lastw = {}
        self.readers = {}
        self.dtot = {}
        self.dry = False

    def _waits(self, eng, reads, writes):
        E = self.E[eng]
        need = {}

        def add(tok):
            if tok is None:
                return
            k, v = tok
            if need.get(k, 0) < v:
                need[k] = v
        for r in reads:
            add(self.lastw.get(r))
            if isinstance(r, tuple) and r[0] == "ps":
                for k, v in self.readers.get(r, {}).items():
                    if k != eng:
                        add((k, v))
        for r in writes:
            w = self.lastw.get(r)
            if w is not None and (w[0] != eng or eng != "tensor"):
                add(w)
            for k, v in self.readers.get(r, {}).items():
                if k != eng or eng != "tensor":
                    add((k, v))
        waits = []
        for k, v in need.items():
            if k == eng:
                assert v <= E.count, ("same-engine dep on un-incremented instr", eng, v, E.count)
            if E.seen.get(k, 0) >= v:
                continue
            E.seen[k] = v
            waits.append((k, v))
        return waits

    def op(self, eng, fn, reads=(), writes=(), inc=True):
        if self.dry:
            return
        if FORCE_INC:
            inc = True
        E = self.E[eng]
        waits = self._waits(eng, reads, writes)
        tokv = E.count + 1
        if inc:
            E.count += 1
        E.prog.append((waits, fn, (eng, 1) if inc else None))
        for r in reads:
            self.readers.setdefault(r, {})[eng] = tokv
        for r in writes:
            self.lastw[r] = (eng, tokv)
            self.readers[r] = {}

    def dma(self, q, fn, semkey, reads=(), writes=()):
        if self.dry:
            return
        E = self.E[q]
        waits = self._waits(q, reads, writes)
        k = "dma:" + semkey
        self.dtot[k] = self.dtot.get(k, 0) + 16
        tot = self.dtot[k]
        E.prog.append((waits, fn, (k, 16)))
        for r in reads:
            self.readers.setdefault(r, {})[k] = tot
        for r in writes:
            self.lastw[r] = (k, tot)
            self.readers[r] = {}

    def wait_all_dma(self, q, semkey):
        k = "dma:" + semkey
        self.E[q].prog.append(([(k, self.dtot[k])], None, None))

    def sem_names(self):
        return list(self.ENGS) + sorted(self.dtot.keys())

    def replay(self, eng, e, sems):
        for waits, fn, inc in self.E[eng].prog:
            for k, v in waits:
                e.wait_ge(sems[k], v)
            if fn is None:
                continue
            ins = fn(e)
            if inc is not None:
                ins.then_inc(sems[inc[0]], inc[1])


LAYER_KIND = ["ssd", "sc", "ssd", "sc"]


class Builder:
    def __init__(self, S_len, layers, final_norm):
        self.S_len = S_len
        self.NT = S_len // T
        self.layers = list(layers)
        self.final_norm = final_norm
        self.nc = bass.Bass("TRN2", target_bir_lowering=False)
        self.S = Sched()
        self.poff, self.pn = pvec_layout()

    def sb(self, name, shape, dt):
        return self.stack.enter_context(self.nc.sbuf_tensor(name, list(shape), dt))

    def build(self):
        nc = self.nc
        S = self.S
        with contextlib.ExitStack() as stack:
            self.stack = stack
            self.x_d = nc.dram_tensor("x", [D, self.S_len], F32, kind="ExternalInput").ap()
            self.y_d = nc.dram_tensor("y", [D, self.S_len], F32, kind="ExternalOutput").ap()
            self.pv_d = nc.dram_tensor("pvec", [P, self.pn], F32, kind="ExternalInput").ap()
            self.w_d = {}
            self._plans = {k: group_plan(unit_cols(k))[1] for k in ("ffn", "sc", "ssd")}
            self._wtile = 0
            self._wcur = None
            self.wb_d = {}
            self.cast_done = set()
            for (l, do_mix, do_ffn) in self.layers:
                kind = LAYER_KIND[l]
                nm = sum(group_sizes(kind)) * P
                nf = sum(group_sizes("ffn")) * P
                if do_mix:
                    self.w_d[("m", l)] = nc.dram_tensor(f"wm{l}", [nm], F32, kind="ExternalInput").ap()
                    self.wb_d[("m", l)] = nc.dram_tensor(f"wbm{l}", [nm], BF16).ap()
                if do_ffn:
                    self.w_d[("f", l)] = nc.dram_tensor(f"wf{l}", [nf], F32, kind="ExternalInput").ap()
                    self.wb_d[("f", l)] = nc.dram_tensor(f"wbf{l}", [nf], BF16).ap()

            self.xres = self.sb("xres", [P, KC, T], F32)
            self.h = self.sb("h", [P, KC, T], BF16)
            self.big = self.sb("big", [P, FJ, T], BF16)
            self.wslot = [self.sb(f"wslot{i}", [P, GBMAX * P], BF16) for i in range(NSLOT)]
            self.pvec = self.sb("pvec_sb", [P, self.pn], F32)
            self.gam = self.sb("gam", [P, 9 * KC], F32)
            self.carry_ffn = self.sb("carry_ffn", [P, DEPTH, 2 * FJ, 2], F32)
            self.carry_sc = self.sb("carry_sc", [P, 2, KC, 2], F32)
            self.rstd = self.sb("rstd", [P, T], F32)
            self.neghalf = self.sb("neghalf", [P, 1], F32)
            self.ones_bf = self.sb("ones_bf", [P, P], BF16)
            self.has_ssd = any(LAYER_KIND[l] == "ssd" and dm for (l, dm, df) in self.layers)
            if self.has_ssd:
                self.states = self.sb("states", [P, 2, NG, T], F32)
                self.state_bf = self.sb("state_bf", [P, T], BF16)
                self.carry_ssd = self.sb("carry_ssd", [P, 2, 48, 3], F32)
                self.ident_bf = self.sb("ident_bf", [P, P], BF16)
                self.mle_bf = self.sb("mle_bf", [P, P], BF16)
                self.mgt_bf = self.sb("mgt_bf", [P, P], BF16)
                self.mle_f = self.sb("mle_f", [P, P], F32)
                self.aneg = self.sb("aneg", [P, 2, NH], F32)
                self.gnw = self.sb("gnw", [P, 2, 32], F32)
                self.dtt = self.sb("dtt", [P, 4, NH], F32)
                self.dat = self.sb("dat", [P, 4, NH], F32)
                self.da_bf = self.sb("da_bf", [P, 4, NH], BF16)
                self.expcs = self.sb("expcs", [P, 4, NH], F32)
                self.dte = self.sb("dte", [P, 4, NH], F32)
                self.cdb = self.sb("cdb", [P, 4, NH], F32)
                self.diag4 = self.sb("diag4", [P, 4 * P], BF16)
                self.ssq = self.sb("ssq", [P, 8], F32)
                self.s_cbt = self.sb("s_cbt", [P, P], BF16)
                self.s_btm = self.sb("s_btm", [P, P], BF16)
                self.s_xdt = self.sb("s_xdt", [P, T], BF16)
                self.s_xdtd = self.sb("s_xdtd", [P, T], BF16)
                self.s_yn = self.sb("s_yn", [P, T], BF16)
                self.s_rd = self.sb("s_rd", [P, 2 * T], BF16)
                self.s_ee = self.sb("s_ee", [P, 2 * T], BF16)
                self.s_mt = self.sb("s_mt", [P, 2 * T], BF16)
                self.s_mt2 = self.sb("s_mt2", [P, 2 * T], BF16)
                self.s_xdt2 = self.sb("s_xdt2", [P, T], BF16)
                self.s_xdtd2 = self.sb("s_xdtd2", [P, T], BF16)
                self.s_btm2 = self.sb("s_btm2", [P, P], BF16)
            self.n32 = 7
            self.n16 = 4
            self.t32 = [self.sb(f"t32_{i}", [P, T + 4], F32) for i in range(self.n32)]
            self.t16 = [self.sb(f"t16_{i}", [P, T], BF16) for i in range(self.n16)]
            self.i32 = 0
            self.i16 = 0
            self.psb = [stack.enter_context(nc.psum_tensor(f"ps{i}", [P, T], F32)) for i in range(8)]
            self.ips = 0

            S.dry = True
            self.wreq = []
            self.emit_all()
            S.dry = False
            self.worder = list(self.wreq)
            self.wreq = []
            self.wnext_dma = 0
            self.i32 = self.i16 = self.ips = 0
            self.emit_all()
            S.wait_all_dma("sync", "out")

            names = S.sem_names()
            sems = {n: stack.enter_context(nc.semaphore(n.replace(":", "_"))) for n in names}
            with nc.Block() as block:
                @block.tensor
                def _(e):
                    S.replay("tensor", e, sems)

                @block.vector
                def _(e):
                    S.replay("vector", e, sems)

                @block.scalar
                def _(e):
                    S.replay("scalar", e, sems)

                @block.gpsimd
                def _(e):
                    S.replay("gpsimd", e, sems)

                @block.sync
                def _(e):
                    S.replay("sync", e, sems)
        return nc

    def tmp32(self):
        i = self.i32
        self.i32 = (i + 1) % self.n32
        return self.t32[i], ("t32", i)

    def tmp16(self):
        i = self.i16
        self.i16 = (i + 1) % self.n16
        return self.t16[i], ("t16", i)

    def psum(self):
        i = self.ips
        self.ips = (i + 1) % 8
        return self.psb[i], ("ps", i)

    def pcol(self, name, i, n=1):
        o, k = self.poff[name]
        return self.pvec[:, o + i:o + i + n]

    def wget(self, part, l, gi):
        req = (part, l, gi)
        if self.S.dry:
            self.wreq.append(req)
            return self.wslot[0], ("w", 0)
        idx = len(self.wreq)
        assert self.worder[idx] == req
        self.wreq.append(req)
        while self.wnext_dma < len(self.worder) and self.wnext_dma <= idx + NSLOT - 1:
            self._wdma(self.wnext_dma)
            self.wnext_dma += 1
        s = idx % NSLOT
        return self.wslot[s], ("w", s)

    def wunit(self, part, l, ui):
        kind = "ffn" if part == "f" else LAYER_KIND[l]
        gi, coff = group_plan(unit_cols(kind))[1][ui] if kind not in self._plans else self._plans[kind][ui]
        cur = getattr(self, "_wcur", None)
        if cur is None or cur[0] != (part, l, gi, self._wtile):
            slot, slotk = self.wget(part, l, gi)
            self._wcur = ((part, l, gi, self._wtile), slot, slotk)
        _, slot, slotk = self._wcur
        return slot, slotk, coff // P, coff

    def _wdma(self, i):
        part, l, gi = self.worder[i]
        kind = "ffn" if part == "f" else LAYER_KIND[l]
        sizes = group_sizes(kind)
        off = sum(sizes[:gi]) * P
        n = sizes[gi]
        s = i % NSLOT
        dst = self.wslot[s][:, 0:n]
        scr = self.wb_d[(part, l)][off:off + n * P].rearrange("(p n) -> p n", p=P)
        gkey = ("wbg", part, l, gi)
        if (part, l, gi) not in self.wseen:
            self.wseen.add((part, l, gi))
            src = self.w_d[(part, l)][off:off + n * P].rearrange("(p n) -> p n", p=P)
            self.S.dma("gpsimd", lambda e, dst=dst, src=src: e.dma_start(out=dst, in_=src),
                       f"wsw{s}", reads=(), writes=[("w", s)])
            if self.NT > 1:
                self.S.dma("sync", lambda e, dst=dst, scr=scr: e.dma_start(out=scr, in_=dst),
                           f"wst{s}", reads=[("w", s)], writes=[gkey])
        else:
            self.S.dma("sync", lambda e, dst=dst, scr=scr: e.dma_start(out=dst, in_=scr),
                       f"w{s}", reads=[gkey], writes=[("w", s)])

    def emit_cast(self, part, l):
        if (part, l) in self.cast_done or (part, l) not in self.w_d:
            return
        self.cast_done.add((part, l))
        src = self.w_d[(part, l)]
        dst = self.wb_d[(part, l)]
        n = src.shape[0]
        CH = P * 8192
        off = 0
        while off < n:
            m = min(CH, n - off)
            a = src[off:off + m].rearrange("(p n) -> p n", p=P)
            b = dst[off:off + m].rearrange("(p n) -> p n", p=P)
            self.S.dma("gpsimd", lambda e, a=a, b=b: e.dma_start(out=b, in_=a), f"cast{part}{l}",
                       writes=[("wb", part, l)])
            off += m

    def emit_all(self):
        S = self.S
        if not S.dry:
            self.emit_setup()
        parts = []
        for (l, do_mix, do_ffn) in self.layers:
            if do_mix:
                parts.append(("m", l))
            if do_ffn:
                parts.append(("f", l))
        if not S.dry:
            self.wseen = set()
        self._wcur = None
        for ti in range(self.NT):
            self._wtile = ti
            t0 = ti * T
            xkeys = [("x", c) for c in range(KC)]
            src = self.x_d.rearrange("(c p) t -> p c t", p=P)[:, :, t0:t0 + T]
            S.dma("sync", lambda e, src=src: e.dma_start(out=self.xres[:, :, :], in_=src), "xin",
                  writes=xkeys)
            for (l, do_mix, do_ffn) in self.layers:
                kind = LAYER_KIND[l]
                if do_mix:
                    self.rmsnorm(self.gcol(f"mixnorm{l}"))
                    if kind == "ssd":
                        self.ssd_mixer(l, ti)
                    else:
                        self.sc_mixer(l, ti)
                if do_ffn:
                    self.rmsnorm(self.gcol(f"ffnnorm{l}"))
                    self.ffn(l, ti)
            if self.final_norm:
                self.rmsnorm(self.gcol("finalnorm"), inplace=True)
            dst = self.y_d.rearrange("(c p) t -> p c t", p=P)[:, :, t0:t0 + T]
            S.dma("sync", lambda e, dst=dst: e.dma_start(out=dst, in_=self.xres[:, :, :]), "out",
                  reads=xkeys)

    def gcol(self, name):
        names = []
        for l in range(DEPTH):
            names += [f"mixnorm{l}", f"ffnnorm{l}"]
        names.append("finalnorm")
        return names.index(name) * KC

    def emit_setup(self):
        S = self.S
        S.dma("sync", lambda e: e.dma_start(out=self.pvec[:, :], in_=self.pv_d[:, :]), "pv", writes=["pvec"])
        S.op("gpsimd", lambda e: e.memset(self.neghalf[:, :], -0.5), writes=["neghalf"])
        S.op("gpsimd", lambda e: e.memset(self.ones_bf[:, :], 1.0), writes=["ones"])
        S.op("gpsimd", lambda e: e.memset(self.carry_ffn[:, :, :, :], 0.0), writes=["carry_ffn"])
        S.op("gpsimd", lambda e: e.memset(self.carry_sc[:, :, :, :], 0.0), writes=["carry_sc"])
        names = []
        for l in range(DEPTH):
            names += [f"mixnorm{l}", f"ffnnorm{l}"]
        names.append("finalnorm")
        for i, nme in enumerate(names):
            o, k = self.poff[nme]
            S.op("vector", lambda e, i=i, o=o: e.tensor_scalar(
                out=self.gam[:, i * KC:(i + 1) * KC], in0=self.pvec[:, o:o + KC],
                scalar1=float(np.sqrt(D)), scalar2=None, op0=ALU.mult),
                reads=["pvec"], writes=["gam"])
        self.setup_ssd()

    def setup_ssd(self):
        if not self.has_ssd:
            return
        S = self.S
        g = "gpsimd"
        S.op(g, lambda e: e.memset(self.states[:, :, :, :], 0.0), writes=[("state", j, gg) for j in range(2) for gg in range(NG)])
        S.op(g, lambda e: e.memset(self.carry_ssd[:, :, :, :], 0.0), writes=["carry_ssd"])
        S.op(g, lambda e: e.memset(self.mle_f[:, :], 1.0), writes=["mle_f"])
        S.op(g, lambda e: e.affine_select(out=self.mle_f[:, :], in_=self.mle_f[:, :], pattern=[[1, P]],
                                          compare_op=ALU.is_ge, fill=0.0, base=0, channel_multiplier=-1),
             reads=["mle_f"], writes=["mle_f"])
        S.op(g, lambda e: e.tensor_copy(out=self.mle_bf[:, :], in_=self.mle_f[:, :]), reads=["mle_f"], writes=["mle_bf"])
        S.op(g, lambda e: e.tensor_scalar(out=self.mgt_bf[:, :], in0=self.mle_f[:, :], scalar1=-1.0, scalar2=1.0,
                                          op0=ALU.mult, op1=ALU.add), reads=["mle_f"], writes=["mgt_bf"])
        S.op(g, lambda e: e.memset(self.ident_bf[:, :], 1.0), writes=["ident"])
        S.op(g, lambda e: e.affine_select(out=self.ident_bf[:, :], in_=self.ident_bf[:, :], pattern=[[-1, P]],
                                          compare_op=ALU.is_equal, fill=0.0, base=0, channel_multiplier=1),
             reads=["ident"], writes=["ident"])
        for j in range(2):
            o, _ = self.poff[f"alog{j}"]
            S.op("scalar", lambda e, j=j, o=o: e.activation(out=self.aneg[:, j, :], in_=self.pvec[:, o:o + NH], func=AF.Exp),
                 reads=["pvec"], writes=[("aneg", j)])
            S.op("vector", lambda e, j=j: e.tensor_scalar(out=self.aneg[:, j, :], in0=self.aneg[:, j, :], scalar1=-1.0,
                                                          scalar2=None, op0=ALU.mult),
                 reads=[("aneg", j)], writes=[("aneg", j)])
            o, _ = self.poff[f"ssdnormw{j}"]
            S.op("vector", lambda e, j=j, o=o: e.tensor_scalar(out=self.gnw[:, j, :], in0=self.pvec[:, o:o + 32],
                                                               scalar1=float(np.sqrt(512.0)), scalar2=None, op0=ALU.mult),
                 reads=["pvec"], writes=["gnw"])

    def rmsnorm(self, gcol0, inplace=False):
        S = self.S
        ps, psk = self.psum()
        for c in range(KC):
            sq, sqk = self.tmp16()
            S.op("scalar", lambda e, c=c, sq=sq: e.activation(out=sq[:, 0:T], in_=self.xres[:, c, :], func=AF.Square),
                 reads=[("x", c)], writes=[sqk])
            S.op("tensor", lambda e, c=c, sq=sq: e.matmul(ps[:, :], self.ones_bf[:, :], sq[:, 0:T],
                                                          start=(c == 0), stop=(c == KC - 1)),
                 reads=[sqk, "ones"], writes=[psk], inc=True)
        t, tk = self.tmp32()
        S.op("scalar", lambda e: e.activation(out=t[:, 0:T], in_=ps[:, :], func=AF.Ln, bias=float(D * EPS)),
             reads=[psk], writes=[tk])
        S.op("scalar", lambda e: e.activation(out=self.rstd[:, :], in_=t[:, 0:T], func=AF.Exp, scale=-0.5),
             reads=[tk], writes=["rstd"])
        for c in range(KC):
            if inplace:
                out = self.xres[:, c, :]
                wk = ("x", c)
            else:
                out = self.h[:, c, :]
                wk = ("h", c)
            S.op("vector", lambda e, c=c, out=out: e.scalar_tensor_tensor(
                out=out, in0=self.xres[:, c, :], scalar=self.gam[:, gcol0 + c:gcol0 + c + 1],
                in1=self.rstd[:, :], op0=ALU.mult, op1=ALU.mult),
                reads=[("x", c), "rstd", "gam"], writes=[wk])

    def mm_fm(self, slot, slotk, blk0, nk, rhs_fn, rhs_keys, ps, psk, k0=0, ktot=None):
        S = self.S
        if ktot is None:
            ktot = nk
        for k in range(nk):
            kg = k0 + k
            S.op("tensor", lambda e, k=k, kg=kg: e.matmul(ps[:, :], slot[:, (blk0 + k) * P:(blk0 + k + 1) * P], rhs_fn(kg),
                                                          start=(kg == 0), stop=(kg == ktot - 1)),
                 reads=[slotk] + list(rhs_keys), writes=[psk], inc=(k == nk - 1))

    def conv_chain(self, specs):
        S = self.S
        maxw = max(s["width"] for s in specs)
        for step in range(maxw):
            for s in specs:
                wd = s["width"]
                if step >= wd:
                    continue
                if step == 0 and s.get("first_done"):
                    continue
                tap = wd - 1 - step
                shift = step
                src = s["ub"][:, 4 - shift:4 - shift + T]
                acc = s["acc"]
                if step == 0:
                    if s["b"] is not None:
                        S.op("vector", lambda e, src=src, acc=acc, s=s, tap=tap: e.tensor_scalar(
                            out=acc[:, 0:T], in0=src, scalar1=s["w"][tap], scalar2=s["b"], op0=ALU.mult, op1=ALU.add),
                            reads=[s["ubk"], (s["ubk"], "h"), "pvec"], writes=[s["acck"]])
                    else:
                        S.op("vector", lambda e, src=src, acc=acc, s=s, tap=tap: e.tensor_scalar(
                            out=acc[:, 0:T], in0=src, scalar1=s["w"][tap], scalar2=None, op0=ALU.mult),
                            reads=[s["ubk"], (s["ubk"], "h"), "pvec"], writes=[s["acck"]])
                else:
                    S.op("vector", lambda e, src=src, acc=acc, s=s, tap=tap: e.scalar_tensor_tensor(
                        out=acc[:, 0:T], in0=src, scalar=s["w"][tap], in1=acc[:, 0:T], op0=ALU.mult, op1=ALU.add),
                        reads=[s["ubk"], (s["ubk"], "h"), s["acck"], "pvec"], writes=[s["acck"]])

    def carry_io(self, ub, ubk, carry_ap, carryk, width):
        S = self.S
        hw = width - 1
        S.op("gpsimd", lambda e: e.tensor_copy(out=ub[:, 4 - hw:4], in_=carry_ap), reads=[carryk],
             writes=[ubk, (ubk, "h")])

    def carry_save(self, ub, ubk, carry_ap, carryk, width):
        S = self.S
        hw = width - 1
        S.op("gpsimd", lambda e: e.tensor_copy(out=carry_ap, in_=ub[:, 4 + T - hw:4 + T]), reads=[ubk, (ubk, "h")],
             writes=[carryk])

    def ffn(self, l, ti):
        S = self.S
        hkeys = [("h", k) for k in range(KC)]
        o_cv, _ = self.poff[f"ffnconv{l}"]
        for jp in range(FJ):
            for jj in range(1):
                j = jp
                specs = []
                for which in range(2):
                    ch = which * FJ + j
                    ps, psk = self.psum()
                    slot, slotk, b0, _ = self.wunit("f", l, 2 * j + which)
                    self.mm_fm(slot, slotk, b0, KC, lambda k: self.h[:, k, :], hkeys, ps, psk)
                    ub, ubk = self.tmp32()
                    acc, acck = self.tmp32()
                    car = self.carry_ffn[:, l, ch, :]
                    cark = ("carry_ffn", l, ch)
                    if ti == 0 and ("carry_ffn", l, ch) not in S.lastw and not S.dry:
                        S.lastw[cark] = S.lastw.get("carry_ffn")
                    self.carry_io(ub, ubk, car, cark, 3)
                    S.op("scalar", lambda e, ub=ub, ps=ps: e.activation(out=ub[:, 4:4 + T], in_=ps[:, :], func=AF.Copy),
                         reads=[psk], writes=[ubk])
                    self.carry_save(ub, ubk, car, cark, 3)
                    base = o_cv + ch * 4
                    S.op("scalar", lambda e, acc=acc, ps=ps, base=base: e.activation(
                        out=acc[:, 0:T], in_=ps[:, :], func=AF.Identity,
                        scale=self.pvec[:, base + 2:base + 3], bias=self.pvec[:, base + 3:base + 4]),
                        reads=[psk, "pvec"], writes=[acck])
                    specs.append(dict(ub=ub, ubk=ubk, acc=acc, acck=acck,
                                      w=[self.pvec[:, base + i:base + i + 1] for i in range(3)],
                                      b=self.pvec[:, base + 3:base + 4], width=3, first_done=True))
                self.conv_chain([dict(s, ubk=s["ubk"]) for s in specs])
                sg, sgk = self.tmp32()
                S.op("scalar", lambda e, sg=sg, a=specs[0]["acc"]: e.activation(out=sg[:, 0:T], in_=a[:, 0:T], func=AF.Silu),
                     reads=[specs[0]["acck"]], writes=[sgk])
                S.op("gpsimd", lambda e, sg=sg, a=specs[1]["acc"], j=j: e.tensor_tensor(
                    out=self.big[:, j, :], in0=sg[:, 0:T], in1=a[:, 0:T], op=ALU.mult),
                    reads=[sgk, specs[1]["acck"]], writes=[("big", j)])
        bkeys = [("big", j) for j in range(FJ)]
        for m in range(KC):
            ps, psk = self.psum()
            for hf in range(2):
                slot, slotk, b0, _ = self.wunit("f", l, 2 * FJ + 2 * m + hf)
                self.mm_fm(slot, slotk, b0, FJ // 2, lambda k: self.big[:, k, :], bkeys, ps, psk, k0=hf * (FJ // 2), ktot=FJ)
            S.op("vector", lambda e, m=m, ps=ps: e.tensor_tensor(out=self.xres[:, m, :], in0=self.xres[:, m, :],
                                                                 in1=ps[:, :], op=ALU.add),
                 reads=[("x", m), psk], writes=[("x", m)])

    def sc_mixer(self, l, ti):
        S = self.S
        jl = l // 2
        hkeys = [("h", k) for k in range(KC)]
        o_cv, _ = self.poff[f"scconv{jl}"]
        for j in range(KC):
            pss = []
            for which in range(3):
                ps, psk = self.psum()
                slot, slotk, b0, _ = self.wunit("m", l, 3 * j + which)
                self.mm_fm(slot, slotk, b0, KC, lambda k: self.h[:, k, :], hkeys, ps, psk)
                pss.append((ps, psk))
            cgs, cgk = self.tmp32()
            S.op("scalar", lambda e, cgs=cgs, ps=pss[1][0]: e.activation(out=cgs[:, 0:T], in_=ps[:, :], func=AF.Copy),
                 reads=[pss[1][1]], writes=[cgk])
            ub, ubk = self.tmp32()
            acc, acck = self.tmp32()
            car = self.carry_sc[:, jl, j, :]
            cark = ("carry_sc", jl, j)
            if ti == 0 and cark not in S.lastw and not S.dry:
                S.lastw[cark] = S.lastw.get("carry_sc")
            self.carry_io(ub, ubk, car, cark, 3)
            S.op("vector", lambda e, ub=ub, cgs=cgs, ps=pss[2][0]: e.tensor_tensor(
                out=ub[:, 4:4 + T], in0=cgs[:, 0:T], in1=ps[:, :], op=ALU.mult),
                reads=[cgk, pss[2][1]], writes=[ubk])
            self.carry_save(ub, ubk, car, cark, 3)
            base = o_cv + j * 3
            self.conv_chain([dict(ub=ub, ubk=ubk, acc=acc, acck=acck,
                                  w=[self.pvec[:, base + i:base + i + 1] for i in range(3)], b=None, width=3)])
            S.op("vector", lambda e, acc=acc, ps=pss[0][0], j=j: e.tensor_tensor(
                out=self.big[:, j, :], in0=acc[:, 0:T], in1=ps[:, :], op=ALU.mult),
                reads=[acck, pss[0][1]], writes=[("big", j)])
        bkeys = [("big", j) for j in range(KC)]
        for mg in range(8):
            for mi in range(2):
                m = mg * 2 + mi
                ps, psk = self.psum()
                slot, slotk, b0, _ = self.wunit("m", l, 3 * KC + m)
                self.mm_fm(slot, slotk, b0, KC, lambda k: self.big[:, k, :], bkeys, ps, psk)
                S.op("vector", lambda e, m=m, ps=ps: e.tensor_tensor(out=self.xres[:, m, :], in0=self.xres[:, m, :],
                                                                     in1=ps[:, :], op=ALU.add),
                     reads=[("x", m), psk], writes=[("x", m)])

    def ssd_mixer(self, l, ti):
        S = self.S
        jl = l // 2
        hkeys = [("h", k) for k in range(KC)]
        o_cv, _ = self.poff[f"ssdconv{jl}"]
        o_db, _ = self.poff[f"dtbias{jl}"]
        o_d, _ = self.poff[f"ssdd{jl}"]
        big = self.big
        bc = lambda ap, shape: ap.to_broadcast(shape)
        def dt_phase():
            slot, slotk, _, co = self.wunit("m", l, 0)
            ps, psk = self.psum()
            for c in range(4):
                for k in range(KC):
                    S.op("tensor", lambda e, k=k, c=c, ps=ps, slot=slot, co=co: e.matmul(
                        ps[:, c * NH:(c + 1) * NH], self.h[:, k, c * P:(c + 1) * P], slot[:, co + k * NH:co + (k + 1) * NH],
                        start=(k == 0), stop=(k == KC - 1)), reads=[slotk] + hkeys, writes=[psk], inc=(k == KC - 1))
            t1, t1k = self.tmp32()
            v3 = lambda ap: ap.rearrange("p (c h) -> p c h", c=4)
            S.op("vector", lambda e, ps=ps, t1=t1: e.tensor_tensor(
                out=v3(t1[:, 0:4 * NH]), in0=v3(ps[:, 0:4 * NH]),
                in1=bc(self.pvec[:, o_db:o_db + NH].unsqueeze(1), [P, 4, NH]), op=ALU.add),
                reads=[psk, "pvec"], writes=[t1k])
            S.op("scalar", lambda e, t1=t1: e.activation(out=t1[:, 0:4 * NH], in_=t1[:, 0:4 * NH], func=AF.Exp),
                 reads=[t1k], writes=[t1k])
            S.op("scalar", lambda e, t1=t1: e.activation(out=self.dtt[:, :, :].rearrange("p c h -> p (c h)"), in_=t1[:, 0:4 * NH],
                                                         func=AF.Ln, bias=1.0),
                 reads=[t1k], writes=[("dtt", c) for c in range(4)])
            S.op("vector", lambda e: e.tensor_tensor(out=self.dat[:, :, :], in0=self.dtt[:, :, :],
                                                     in1=bc(self.aneg[:, jl, :].unsqueeze(1), [P, 4, NH]), op=ALU.mult),
                 reads=[("dtt", c) for c in range(4)] + [("aneg", jl)], writes=[("dat", c) for c in range(4)])
            S.op("vector", lambda e: e.tensor_copy(out=self.da_bf[:, :, :], in_=self.dat[:, :, :]),
                 reads=[("dat", c) for c in range(4)], writes=[("da_bf", c) for c in range(4)])
            for (lhs, lhsk, dst, dstk) in ((self.mle_bf, "mle_bf", self.expcs, "expcs"),
                                           (self.mgt_bf, "mgt_bf", self.dte, "dte"),
                                           (self.ones_bf, "ones", self.cdb, "cdb")):
                ps2, ps2k = self.psum()
                for c in range(4):
                    S.op("tensor", lambda e, ps2=ps2, lhs=lhs, c=c: e.matmul(ps2[:, c * NH:(c + 1) * NH], lhs[:, :], self.da_bf[:, c, :],
                                                                             start=True, stop=True),
                         reads=[lhsk] + [("da_bf", cc) for cc in range(4)], writes=[ps2k], inc=(c == 3))
                S.op("scalar", lambda e, ps2=ps2, dst=dst: e.activation(out=dst[:, :, :].rearrange("p c h -> p (c h)"),
                                                                        in_=ps2[:, 0:4 * NH], func=AF.Exp),
                     reads=[ps2k], writes=[(dstk, c) for c in range(4)])
        ZS, XS, BG, CG = 32, 36, 40, 41
        for g in range(NG):
            stk = ("state", jl, g)
            S.op("scalar", lambda e, g=g: e.activation(out=self.state_bf[:, :], in_=self.states[:, jl, g, :], func=AF.Copy),
                 reads=[stk], writes=["state_bf"])
            for r in range(4):
                S.op("vector", lambda e, r=r, g=g: e.tensor_scalar(
                    out=self.diag4[:, r * P:(r + 1) * P], in0=self.ident_bf[:, :],
                    scalar1=self.pvec[:, o_d + 4 * g + r:o_d + 4 * g + r + 1], scalar2=None, op0=ALU.mult),
                    reads=["ident", "pvec"], writes=["diag4"])
            for half in range(2):
                for rr in range(3):
                    r = half * 3 + rr
                    ci = 6 * g + r
                    ps, psk = self.psum()
                    slot, slotk, b0, _ = self.wunit("m", l, 1 + 10 * g + r)
                    self.mm_fm(slot, slotk, b0, KC, lambda k: self.h[:, k, :], hkeys, ps, psk)
                    ub, ubk = self.tmp32()
                    acc, acck = self.tmp32()
                    car = self.carry_ssd[:, jl, ci, :]
                    cark = ("carry_ssd", jl, ci)
                    if ti == 0 and cark not in S.lastw and not S.dry:
                        S.lastw[cark] = S.lastw.get("carry_ssd")
                    self.carry_io(ub, ubk, car, cark, 4)
                    S.op("scalar", lambda e, ub=ub, ps=ps: e.activation(out=ub[:, 4:4 + T], in_=ps[:, :], func=AF.Copy),
                         reads=[psk], writes=[ubk])
                    self.carry_save(ub, ubk, car, cark, 4)
                    base = o_cv + ci * 5
                    self.conv_chain([dict(ub=ub, ubk=ubk, acc=acc, acck=acck,
                                          w=[self.pvec[:, base + i:base + i + 1] for i in range(4)],
                                          b=self.pvec[:, base + 4:base + 5], width=4)])
                    th, thk = self.tmp32()
                    S.op("scalar", lambda e, th=th, acc=acc: e.activation(out=th[:, 0:T], in_=acc[:, 0:T], func=AF.Tanh, scale=0.5),
                         reads=[acck], writes=[thk])
                    S.op("gpsimd", lambda e, th=th: e.tensor_scalar(out=th[:, 0:T], in0=th[:, 0:T], scalar1=0.5, scalar2=0.5,
                                                                    op0=ALU.mult, op1=ALU.add), reads=[thk], writes=[thk])
                    dsti = XS + r if r < 4 else (BG if r == 4 else CG)
                    S.op("gpsimd", lambda e, th=th, acc=acc, dsti=dsti: e.tensor_tensor(
                        out=big[:, dsti, :], in0=th[:, 0:T], in1=acc[:, 0:T], op=ALU.mult),
                        reads=[thk, acck], writes=[("big", dsti)])
                    if g == 0 and r == 0:
                        dt_phase()
            zb = [self.psum() for _ in range(4)]
            for half in range(4):
                slot, slotk, _, co = self.wunit("m", l, 1 + 10 * g + 6 + half)
                for c in range(4):
                    for kk in range(4):
                        k = half * 4 + kk
                        S.op("tensor", lambda e, c=c, k=k, kk=kk, slot=slot, zb=zb, co=co: e.matmul(
                            zb[c][0][:, :], self.h[:, k, c * P:(c + 1) * P], slot[:, co + kk * T:co + (kk + 1) * T],
                            start=(k == 0), stop=(k == KC - 1)),
                            reads=[slotk] + hkeys, writes=[zb[c][1]], inc=(kk == 3))
            for c in range(4):
                th, thk = self.tmp32()
                S.op("scalar", lambda e, th=th, c=c, zb=zb: e.activation(out=th[:, 0:T], in_=zb[c][0][:, :], func=AF.Tanh, scale=0.5),
                     reads=[zb[c][1]], writes=[thk])
                S.op("gpsimd", lambda e, th=th: e.tensor_scalar(out=th[:, 0:T], in0=th[:, 0:T], scalar1=0.5, scalar2=0.5,
                                                                op0=ALU.mult, op1=ALU.add), reads=[thk], writes=[thk])
                S.op("vector", lambda e, th=th, c=c, zb=zb: e.tensor_tensor(out=big[:, ZS + c, :], in0=th[:, 0:T], in1=zb[c][0][:, :],
                                                                     op=ALU.mult),
                     reads=[thk, zb[c][1]], writes=[("big", ZS + c)])
            hs = slice(HPG * g, HPG * (g + 1))
            fb = {}

            def front(c, g=g, hs=hs, fb=fb):
                cs = slice(c * P, (c + 1) * P)
                par = c % 2
                ps, psk = self.psum()
                S.op("tensor", lambda e, ps=ps, cs=cs: e.matmul(ps[:, 0:P], big[:, BG, cs], big[:, CG, cs], start=True, stop=True),
                     reads=[("big", BG), ("big", CG)], writes=[psk])
                cbt, cbtk = self.s_cbt, "s_cbt"
                S.op("vector", lambda e, ps=ps, cbt=cbt: e.tensor_tensor(out=cbt[:, 0:P], in0=ps[:, 0:P], in1=self.mle_f[:, :],
                                                                          op=ALU.mult), reads=[psk, "mle_f"], writes=[cbtk])
                pst, pstk = self.psum()
                pstb = pst.bitcast(BF16)
                for r in range(4):
                    S.op("tensor", lambda e, r=r, cs=cs, pstb=pstb: e.transpose(pstb[:, r * P:(r + 1) * P], big[:, XS + r, cs],
                                                                                 self.ident_bf[:, :]),
                         reads=[("big", XS + r), "ident"], writes=[pstk], inc=False)
                S.op("tensor", lambda e, cs=cs, pstb=pstb: e.transpose(pstb[:, 4 * P:5 * P], big[:, BG, cs], self.ident_bf[:, :]),
                     reads=[("big", BG), "ident"], writes=[pstk])
                yield
                xdt, xdtk = (self.s_xdt, "s_xdt") if par == 0 else (self.s_xdt2, "s_xdt2")
                S.op("vector", lambda e, pstb=pstb, xdt=xdt, c=c, hs=hs: e.tensor_tensor(
                    out=xdt[:, 0:T].rearrange("p (h d) -> p h d", h=HPG),
                    in0=pstb[:, 0:T].rearrange("p (h d) -> p h d", h=HPG),
                    in1=bc(self.dtt[:, c, hs].unsqueeze(2), [P, HPG, HD]), op=ALU.mult),
                    reads=[pstk, ("dtt", c)], writes=[xdtk])
                btm, btmk = (self.s_btm, "s_btm") if par == 0 else (self.s_btm2, "s_btm2")
                S.op("scalar", lambda e, pstb=pstb, btm=btm: e.activation(out=btm[:, 0:P], in_=pstb[:, 4 * P:5 * P], func=AF.Copy),
                     reads=[pstk], writes=[btmk])
                yield
                xdtd, xdtdk = (self.s_xdtd, "s_xdtd") if par == 0 else (self.s_xdtd2, "s_xdtd2")
                S.op("gpsimd", lambda e, xdt=xdt, xdtd=xdtd, c=c, hs=hs: e.tensor_tensor(
                    out=xdtd[:, 0:T].rearrange("p (h d) -> p h d", h=HPG),
                    in0=xdt[:, 0:T].rearrange("p (h d) -> p h d", h=HPG),
                    in1=bc(self.dte[:, c, hs].unsqueeze(2), [P, HPG, HD]), op=ALU.mult),
                    reads=[xdtk, ("dte", c)], writes=[xdtdk])
                rd, rdk = self.s_rd, "s_rd"
                S.op("vector", lambda e, rd=rd, c=c, hs=hs: e.tensor_tensor(
                    out=rd[:, :].rearrange("p (h l) -> p h l", h=HPG),
                    in0=bc(self.mle_bf[:, :].unsqueeze(1), [P, HPG, P]),
                    in1=bc(self.dat[:, c, hs].unsqueeze(2), [P, HPG, P]), op=ALU.mult),
                    reads=["mle_bf", ("dat", c)], writes=[rdk])
                yield
                ee, eek = self.s_ee, "s_ee"
                for hh2 in range(2):
                    psd, psdk = self.psum()
                    S.op("tensor", lambda e, psd=psd, rd=rd, hh2=hh2: e.matmul(psd[:, :], self.mgt_bf[:, :],
                                                                               rd[:, hh2 * T:(hh2 + 1) * T], start=True, stop=True),
                         reads=["mgt_bf", rdk], writes=[psdk])
                    S.op("scalar", lambda e, psd=psd, ee=ee, hh2=hh2: e.activation(out=ee[:, hh2 * T:(hh2 + 1) * T], in_=psd[:, :],
                                                                                   func=AF.Exp),
                         reads=[psdk], writes=[(eek, hh2)])
                yield
                mt, mtk = (self.s_mt, "s_mt") if par == 0 else (self.s_mt2, "s_mt2")
                S.op("vector", lambda e, mt=mt, ee=ee, cbt=cbt: e.tensor_tensor(
                    out=mt[:, :].rearrange("p (h l) -> p h l", h=HPG),
                    in0=ee[:, :].rearrange("p (h l) -> p h l", h=HPG),
                    in1=bc(cbt[:, 0:P].unsqueeze(1), [P, HPG, P]), op=ALU.mult),
                    reads=[(eek, 0), (eek, 1), eek, cbtk], writes=[mtk])
                fb[c] = (xdt, xdtk, btm, btmk, xdtd, xdtdk, mt, mtk)

            def back(c, g=g, hs=hs, fb=fb, stk=stk):
                cs = slice(c * P, (c + 1) * P)
                xdt, xdtk, btm, btmk, xdtd, xdtdk, mt, mtk = fb[c]
                psy, psyk = self.psum()
                for r in range(4):
                    S.op("tensor", lambda e, r=r, cs=cs, psy=psy: e.matmul(
                        psy[:, r * P:(r + 1) * P], big[:, XS + r, cs], self.diag4[:, r * P:(r + 1) * P],
                        start=(r == 0), stop=False, skip_group_check=True),
                        reads=[("big", XS + r), "diag4"], writes=[psyk], inc=False)
                for hh in range(HPG):
                    S.op("tensor", lambda e, hh=hh, psy=psy, mt=mt, xdt=xdt: e.matmul(
                        psy[:, hh * HD:(hh + 1) * HD], mt[:, hh * P:(hh + 1) * P], xdt[:, hh * HD:(hh + 1) * HD],
                        start=False, stop=(hh == HPG - 1), skip_group_check=True),
                        reads=[mtk, xdtk], writes=[psyk], inc=(hh == HPG - 1))
                pso, psok = self.psum()
                S.op("tensor", lambda e, pso=pso, cs=cs: e.matmul(pso[:, :], big[:, CG, cs], self.state_bf[:, :], start=True, stop=True),
                     reads=[("big", CG), "state_bf"], writes=[psok])
                yield
                yt, ytk = self.tmp32()
                S.op("vector", lambda e, pso=pso, yt=yt, c=c, hs=hs: e.tensor_tensor(
                    out=yt[:, 0:T].rearrange("p (h d) -> p h d", h=HPG),
                    in0=pso[:, :].rearrange("p (h d) -> p h d", h=HPG),
                    in1=bc(self.expcs[:, c, hs].unsqueeze(2), [P, HPG, HD]), op=ALU.mult),
                    reads=[psok, ("expcs", c)], writes=[ytk])
                S.op("vector", lambda e, psy=psy, yt=yt: e.tensor_tensor(out=yt[:, 0:T], in0=yt[:, 0:T], in1=psy[:, :], op=ALU.add),
                     reads=[psyk, ytk], writes=[ytk])
                yield
                S.op("gpsimd", lambda e, yt=yt, c=c: e.tensor_tensor(out=yt[:, 0:T], in0=yt[:, 0:T], in1=big[:, ZS + c, :], op=ALU.mult),
                     reads=[ytk, ("big", ZS + c)], writes=[ytk])
                yield
                junk, junkk = self.tmp32()
                col = (g * 4 + c) % 8
                S.op("scalar", lambda e, yt=yt, junk=junk, col=col: e.activation(out=junk[:, 0:T], in_=yt[:, 0:T], func=AF.Square,
                                                                                 accum_out=self.ssq[:, col:col + 1]),
                     reads=[ytk], writes=[junkk, ("ssq", col)])
                yield
                S.op("vector", lambda e, col=col: e.tensor_scalar(out=self.ssq[:, col:col + 1], in0=self.ssq[:, col:col + 1],
                                                                  scalar1=float(512 * EPS), scalar2=None, op0=ALU.add),
                     reads=[("ssq", col)], writes=[("ssq", col)])
                yield
                S.op("gpsimd", lambda e, col=col: e.tensor_tensor(out=self.ssq[:, col:col + 1], in0=self.ssq[:, col:col + 1],
                                                                  in1=self.neghalf[:, 0:1], op=ALU.pow),
                     reads=[("ssq", col), "neghalf"], writes=[("ssq", col)])
                yield
                yn, ynk = self.s_yn, "s_yn"
                S.op("scalar", lambda e, yt=yt, yn=yn, col=col: e.activation(out=yn[:, 0:T], in_=yt[:, 0:T], func=AF.Copy,
                                                                             scale=self.ssq[:, col:col + 1]),
                     reads=[ytk, ("ssq", col)], writes=[ynk])
                yield
                pt3, pt3k = self.psum()
                pt3b = pt3.bitcast(BF16)
                for r in range(4):
                    S.op("tensor", lambda e, r=r, yn=yn, pt3b=pt3b: e.transpose(pt3b[:, r * P:(r + 1) * P], yn[:, r * P:(r + 1) * P],
                                                                                self.ident_bf[:, :]),
                         reads=[ynk, "ident"], writes=[pt3k], inc=(r == 3))
                yield
                for r in range(4):
                    S.op("scalar", lambda e, r=r, pt3b=pt3b, cs=cs, g=g: e.activation(
                        out=big[:, 4 * g + r, cs], in_=pt3b[:, r * P:(r + 1) * P], func=AF.Copy,
                        scale=self.gnw[:, jl, 4 * g + r:4 * g + r + 1]),
                        reads=[pt3k, "gnw"], writes=[("big", 4 * g + r)])

            def stateg(c, g=g, hs=hs, fb=fb, stk=stk):
                xdt, xdtk, btm, btmk, xdtd, xdtdk, mt, mtk = fb[c]
                pss, pssk = self.psum()
                S.op("tensor", lambda e, pss=pss, btm=btm, xdtd=xdtd: e.matmul(pss[:, :], btm[:, 0:P], xdtd[:, 0:T], start=True, stop=True),
                     reads=[btmk, xdtdk], writes=[pssk])
                yield
                S.op("gpsimd", lambda e, g=g, c=c, hs=hs: e.tensor_tensor(
                    out=self.states[:, jl, g, :].rearrange("p (h d) -> p h d", h=HPG),
                    in0=self.states[:, jl, g, :].rearrange("p (h d) -> p h d", h=HPG),
                    in1=bc(self.cdb[:, c, hs].unsqueeze(2), [P, HPG, HD]), op=ALU.mult),
                    reads=[stk, ("cdb", c)], writes=[stk])
                yield
                S.op("vector", lambda e, g=g, pss=pss: e.tensor_tensor(out=self.states[:, jl, g, :], in0=self.states[:, jl, g, :],
                                                                       in1=pss[:, :], op=ALU.add),
                     reads=[stk, pssk], writes=[stk])
                yield
                if c < 3:
                    S.op("scalar", lambda e, g=g: e.activation(out=self.state_bf[:, :], in_=self.states[:, jl, g, :], func=AF.Copy),
                         reads=[stk], writes=["state_bf"])
            for _ in front(0):
                pass
            for c in range(4):
                gens = [back(c), stateg(c)] + ([front(c + 1)] if c < 3 else [])
                while gens:
                    for gg in list(gens):
                        try:
                            next(gg)
                        except StopIteration:
                            gens.remove(gg)
        ykeys = [("big", j) for j in range(32)]
        for m in range(KC):
            ps, psk = self.psum()
            slot, slotk, b0, _ = self.wunit("m", l, 81 + m)
            self.mm_fm(slot, slotk, b0, 32, lambda k: big[:, k, :], ykeys, ps, psk)
            S.op("vector", lambda e, m=m, ps=ps: e.tensor_tensor(out=self.xres[:, m, :], in0=self.xres[:, m, :],
                                                                 in1=ps[:, :], op=ALU.add),
                 reads=[("x", m), psk], writes=[("x", m)])


def pack_layer_weights(inp, l, do_mix=True, do_ffn=True):
    kind = LAYER_KIND[l]
    j = l // 2
    wm = wf = None
    if do_mix:
        if kind == "ssd":
            gm = pack_ssd(inp["ssd_w_in"][j], inp["ssd_w_out"][j])
        else:
            gm = pack_sc(inp["sc_w_in"][j], inp["sc_w_out"][j])
        wm = flat_groups(gm, kind)
    if do_ffn:
        gf = pack_ffn(inp["ffn_w_up"][l], inp["ffn_w_down"][l])
        wf = flat_groups(gf, "ffn")
    return wm, wf


def run_layers(x_fm_list, inp_np, layers, final_norm, S_len):
    b = Builder(S_len, layers, final_norm)
    nc = b.build()
    pv = pack_pvec(inp_np)
    shared = {"pvec": pv}
    for (l, do_mix, do_ffn) in layers:
        wm, wf = pack_layer_weights(inp_np, l, do_mix, do_ffn)
        if do_mix:
            shared[f"wm{l}"] = wm
        if do_ffn:
            shared[f"wf{l}"] = wf
    in_maps = []
    for xf in x_fm_list:
        m = dict(shared)
        m["x"] = np.ascontiguousarray(xf)
        in_maps.append(m)
    res = run_bass_kernel_spmd(nc, in_maps, core_ids=list(range(len(x_fm_list))))
    return [r["y"] for r in res.results]


def kernel(**inputs):
    inp = {k: np.asarray(v) for k, v in inputs.items()}
    x = inp["x"]
    B, S_len, _ = x.shape
    xs = [np.ascontiguousarray(x[b].T) for b in range(B)]
    ys = run_layers(xs, inp, [(l, True, True) for l in range(DEPTH)], True, S_len)
    out = np.stack([np.ascontiguousarray(y.T) for y in ys], axis=0).astype(np.float32)
    return out
```

```python
import contextlib
import numpy as np
import concourse.bass as bass
import concourse.mybir as mybir
from concourse.bass_utils import run_bass_kernel_spmd

F32 = mybir.dt.float32
BF16 = mybir.dt.bfloat16
AF = mybir.ActivationFunctionType
ALU = mybir.AluOpType

P = 128
D = 2048
KC = D // P
T = 512
DEPTH = 4
DFF = 5632
FJ = DFF // P
DIN = 4096
NG = 8
HPG = 8
NH = 64
HD = 64
NST = 128
EPS = 1e-5
GBMAX = 32
NSLOT = 3
import os
FORCE_INC = os.environ.get("FORCE_INC") == "1"


def pvec_layout():
    off = {}
    o = 0

    def add(name, n):
        nonlocal o
        off[name] = (o, n)
        o += n
    for l in range(DEPTH):
        add(f"mixnorm{l}", KC)
        add(f"ffnnorm{l}", KC)
        add(f"ffnconv{l}", 2 * FJ * 4)
    add("finalnorm", KC)
    for j in range(2):
        add(f"ssdconv{j}", 48 * 5)
        add(f"ssdnormw{j}", 32)
        add(f"ssdd{j}", 32)
        add(f"dtbias{j}", NH)
        add(f"alog{j}", NH)
        add(f"scconv{j}", KC * 3)
    return off, o


def ssd_chunk_channels(ci):
    g, r = divmod(ci, 6)
    if r < 4:
        return (4 * g + r) * P
    if r == 4:
        return DIN + g * P
    return DIN + NG * NST + g * P


def pack_pvec(inp):
    off, n = pvec_layout()
    pv = np.zeros((P, n), np.float32)

    def put(name, arr):
        o, k = off[name]
        assert arr.shape == (P, k), (name, arr.shape, k)
        pv[:, o:o + k] = arr
    fm = lambda v: np.ascontiguousarray(v.reshape(-1, P).T)
    for l in range(DEPTH):
        put(f"mixnorm{l}", fm(inp["mix_norm_w"][l]))
        put(f"ffnnorm{l}", fm(inp["ffn_norm_w"][l]))
        w = inp["ffn_conv_w"][l]
        b = inp["ffn_conv_b"][l]
        a = np.stack([fm(w[0]), fm(w[1]), fm(w[2]), fm(b)], axis=2)
        put(f"ffnconv{l}", a.reshape(P, -1))
    put("finalnorm", fm(inp["final_norm_w"]))
    for j in range(2):
        w = inp["ssd_conv_w"][j]
        b = inp["ssd_conv_b"][j]
        a = np.zeros((P, 48, 5), np.float32)
        for ci in range(48):
            c0 = ssd_chunk_channels(ci)
            a[:, ci, 0:4] = w[:, c0:c0 + P].T
            a[:, ci, 4] = b[c0:c0 + P]
        put(f"ssdconv{j}", a.reshape(P, -1))
        put(f"ssdnormw{j}", fm(inp["ssd_norm_w"][j]))
        put(f"ssdd{j}", fm(np.repeat(inp["ssd_d"][j], HD)))
        put(f"dtbias{j}", np.broadcast_to(inp["ssd_dt_bias"][j][None, :], (P, NH)))
        put(f"alog{j}", np.broadcast_to(inp["ssd_a_log"][j][None, :], (P, NH)))
        w = inp["sc_conv_w"][j]
        a = np.stack([fm(w[0]), fm(w[1]), fm(w[2])], axis=2)
        put(f"scconv{j}", a.reshape(P, -1))
    return pv


def _fm_group(W, col_starts):
    kc = W.shape[0] // P
    parts = []
    for cs in col_starts:
        a = W[:, cs:cs + P].reshape(kc, P, P).transpose(1, 0, 2)
        parts.append(a)
    return np.stack(parts, axis=1).reshape(P, -1)


def _tm_group(W, c0, ncols):
    kc = W.shape[0] // P
    return W[:, c0:c0 + ncols].reshape(kc, P, ncols).transpose(1, 0, 2).reshape(P, -1)


def mixer_groups(kind):
    raise NotImplementedError


def pack_ffn(w_up, w_down):
    units = []
    for j in range(FJ):
        units.append(_fm_group(w_up, [j * P]))
        units.append(_fm_group(w_up, [DFF + j * P]))
    hk = (FJ // 2) * P
    for m in range(KC):
        units.append(_fm_group(w_down[0:hk], [m * P]))
        units.append(_fm_group(w_down[hk:], [m * P]))
    return units


def pack_sc(w_in, w_out):
    units = []
    for j in range(KC):
        for which in range(3):
            units.append(_fm_group(w_in, [which * D + j * P]))
    for m in range(KC):
        units.append(_fm_group(w_out, [m * P]))
    return units


def pack_ssd(w_in, w_out):
    units = []
    units.append(_tm_group(w_in, 2 * DIN + 2 * NG * NST, NH))
    for g in range(NG):
        xb = DIN
        for i in range(4):
            units.append(_fm_group(w_in, [xb + (4 * g + i) * P]))
        units.append(_fm_group(w_in, [2 * DIN + g * P]))
        units.append(_fm_group(w_in, [2 * DIN + NG * NST + g * P]))
        for q in range(4):
            units.append(_tm_group(w_in[q * 512:(q + 1) * 512], g * 512, 512))
    for m in range(KC):
        units.append(_fm_group(w_out, [m * P]))
    return units


def unit_cols(kind):
    if kind == "ffn":
        return [KC * P] * (2 * FJ) + [(FJ // 2) * P] * (2 * KC)
    if kind == "sc":
        return [KC * P] * (3 * KC) + [KC * P] * KC
    if kind == "ssd":
        s = [KC * NH]
        for g in range(NG):
            s += [KC * P] * 6 + [4 * 512] * 4
        s += [32 * P] * KC
        return s
    raise ValueError(kind)


def group_plan(cols):
    gsz = []
    loc = []
    cur = 0
    for c in cols:
        if gsz and cur + c <= GBMAX * P:
            loc.append((len(gsz) - 1, cur))
            cur += c
            gsz[-1] = cur
        else:
            gsz.append(c)
            loc.append((len(gsz) - 1, 0))
            cur = c
    return gsz, loc


def flat_groups(units, kind):
    cols = [u.shape[1] for u in units]
    assert cols == unit_cols(kind), (kind, cols[:6], unit_cols(kind)[:6])
    gsz, loc = group_plan(cols)
    groups = [[] for _ in gsz]
    for u, (gi, _) in zip(units, loc):
        groups[gi].append(u)
    flat = np.concatenate([np.ascontiguousarray(np.concatenate(g, axis=1)).reshape(-1) for g in groups]).astype(np.float32)
    return flat


def group_sizes(kind):
    return group_plan(unit_cols(kind))[0]


class _Eng:
    def __init__(self, name):
        self.name = name
        self.count = 0
        self.seen = {}
        self.prog = []


class Sched:
    ENGS = ("tensor", "vector", "scalar", "gpsimd", "sync")

    def __init__(self):
        self.E = {n: _Eng(n) for n in self.ENGS}
        self.lastw = {}
        self.readers = {}
        self.dtot = {}
        self.dry = False

    def _waits(self, eng, reads, writes):
        E = self.E[eng]
        need = {}

        def add(tok):
            if tok is None:
                return
            k, v = tok
            if need.get(k, 0) < v:
                need[k] = v
        for r in reads:
            add(self.lastw.get(r))
            if isinstance(r, tuple) and r[0] == "ps":
                for k, v in self.readers.get(r, {}).items():
                    if k != eng:
                        add((k, v))
        for r in writes:
            w = self.lastw.get(r)
            if w is not None and (w[0] != eng or eng != "tensor"):
                add(w)
            for k, v in self.readers.get(r, {}).items():
                if k != eng or eng != "tensor":
                    add((k, v))
        waits = []
        for k, v in need.items():
            if k == eng:
                assert v <= E.count, ("same-engine dep on un-incremented instr", eng, v, E.count)
            if E.seen.get(k, 0) >= v:
                continue
            E.seen[k] = v
            waits.append((k, v))
        return waits

    def op(self, eng, fn, reads=(), writes=(), inc=True):
        if self.dry:
            return
        if FORCE_INC:
            inc = True
        E = self.E[eng]
        waits = self._waits(eng, reads, writes)
        tokv = E.count + 1
        if inc:
            E.count += 1
        E.prog.append((waits, fn, (eng, 1) if inc else None))
        for r in reads:
            self.readers.setdefault(r, {})[eng] = tokv
        for r in writes:
            self.lastw[r] = (eng, tokv)
            self.readers[r] = {}

    def dma(self, q, fn, semkey, reads=(), writes=()):
        if self.dry:
            return
        E = self.E[q]
        waits = self._waits(q, reads, writes)
        k = "dma:" + semkey
        self.dtot[k] = self.dtot.get(k, 0) + 16
        tot = self.dtot[k]
        E.prog.append((waits, fn, (k, 16)))
        for r in reads:
            self.readers.setdefault(r, {})[k] = tot
        for r in writes:
            self.lastw[r] = (k, tot)
            self.readers[r] = {}

    def wait_all_dma(self, q, semkey):
        k = "dma:" + semkey
        self.E[q].prog.append(([(k, self.dtot[k])], None, None))

    def sem_names(self):
        return list(self.ENGS) + sorted(self.dtot.keys())

    def replay(self, eng, e, sems):
        for waits, fn, inc in self.E[eng].prog:
            for k, v in waits:
                e.wait_ge(sems[k], v)
            if fn is None:
                continue
            ins = fn(e)
            if inc is not None:
                ins.then_inc(sems[inc[0]], inc[1])


LAYER_KIND = ["ssd", "sc", "ssd", "sc"]


class Builder:
    def __init__(self, S_len, layers, final_norm):
        self.S_len = S_len
        self.NT = S_len // T
        self.layers = list(layers)
        self.final_norm = final_norm
        self.nc = bass.Bass("TRN2", target_bir_lowering=False)
        self.S = Sched()
        self.poff, self.pn = pvec_layout()

    def sb(self, name, shape, dt):
        return self.stack.enter_context(self.nc.sbuf_tensor(name, list(shape), dt))

    def build(self):
        nc = self.nc
        S = self.S
        with contextlib.ExitStack() as stack:
            self.stack = stack
            self.x_d = nc.dram_tensor("x", [D, self.S_len], F32, kind="ExternalInput").ap()
            self.y_d = nc.dram_tensor("y", [D, self.S_len], F32, kind="ExternalOutput").ap()
            self.pv_d = nc.dram_tensor("pvec", [P, self.pn], F32, kind="ExternalInput").ap()
            self.w_d = {}
            self._plans = {k: group_plan(unit_cols(k))[1] for k in ("ffn", "sc", "ssd")}
            self._wtile = 0
            self._wcur = None
            self.wb_d = {}
            self.cast_done = set()
            for (l, do_mix, do_ffn) in self.layers:
                kind = LAYER_KIND[l]
                nm = sum(group_sizes(kind)) * P
                nf = sum(group_sizes("ffn")) * P
                if do_mix:
                    self.w_d[("m", l)] = nc.dram_tensor(f"wm{l}", [nm], F32, kind="ExternalInput").ap()
                    self.wb_d[("m", l)] = nc.dram_tensor(f"wbm{l}", [nm], BF16).ap()
                if do_ffn:
                    self.w_d[("f", l)] = nc.dram_tensor(f"wf{l}", [nf], F32, kind="ExternalInput").ap()
                    self.wb_d[("f", l)] = nc.dram_tensor(f"wbf{l}", [nf], BF16).ap()

            self.xres = self.sb("xres", [P, KC, T], F32)
            self.h = self.sb("h", [P, KC, T], BF16)
            self.big = self.sb("big", [P, FJ, T], BF16)
            self.wslot = [self.sb(f"wslot{i}", [P, GBMAX * P], BF16) for i in range(NSLOT)]
            self.pvec = self.sb("pvec_sb", [P, self.pn], F32)
            self.gam = self.sb("gam", [P, 9 * KC], F32)
            self.carry_ffn = self.sb("carry_ffn", [P, DEPTH, 2 * FJ, 2], F32)
            self.carry_sc = self.sb("carry_sc", [P, 2, KC, 2], F32)
            self.rstd = self.sb("rstd", [P, T], F32)
            self.neghalf = self.sb("neghalf", [P, 1], F32)
            self.ones_bf = self.sb("ones_bf", [P, P], BF16)
            self.has_ssd = any(LAYER_KIND[l] == "ssd" and dm for (l, dm, df) in self.layers)
            if self.has_ssd:
                self.states = self.sb("states", [P, 2, NG, T], F32)
                self.state_bf = self.sb("state_bf", [P, T], BF16)
                self.carry_ssd = self.sb("carry_ssd", [P, 2, 48, 3], F32)
                self.ident_bf = self.sb("ident_bf", [P, P], BF16)
                self.mle_bf = self.sb("mle_bf", [P, P], BF16)
                self.mgt_bf = self.sb("mgt_bf", [P, P], BF16)
                self.mle_f = self.sb("mle_f", [P, P], F32)
                self.aneg = self.sb("aneg", [P, 2, NH], F32)
                self.gnw = self.sb("gnw", [P, 2, 32], F32)
                self.dtt = self.sb("dtt", [P, 4, NH], F32)
                self.dat = self.sb("dat", [P, 4, NH], F32)
                self.da_bf = self.sb("da_bf", [P, 4, NH], BF16)
                self.expcs = self.sb("expcs", [P, 4, NH], F32)
                self.dte = self.sb("dte", [P, 4, NH], F32)
                self.cdb = self.sb("cdb", [P, 4, NH], F32)
                self.diag4 = self.sb("diag4", [P, 4 * P], BF16)
                self.ssq = self.sb("ssq", [P, 8], F32)
                self.s_cbt = self.sb("s_cbt", [P, P], BF16)
                self.s_btm = self.sb("s_btm", [P, P], BF16)
                self.s_xdt = self.sb("s_xdt", [P, T], BF16)
                self.s_xdtd = self.sb("s_xdtd", [P, T], BF16)
                self.s_yn = self.sb("s_yn", [P, T], BF16)
                self.s_rd = self.sb("s_rd", [P, 2 * T], BF16)
                self.s_ee = self.sb("s_ee", [P, 2 * T], BF16)
                self.s_mt = self.sb("s_mt", [P, 2 * T], BF16)
                self.s_mt2 = self.sb("s_mt2", [P, 2 * T], BF16)
                self.s_xdt2 = self.sb("s_xdt2", [P, T], BF16)
                self.s_xdtd2 = self.sb("s_xdtd2", [P, T], BF16)
                self.s_btm2 = self.sb("s_btm2", [P, P], BF16)
            self.n32 = 7
            self.n16 = 4
            self.t32 = [self.sb(f"t32_{i}", [P, T + 4], F32) for i in range(self.n32)]
            self.t16 = [self.sb(f"t16_{i}", [P, T], BF16) for i in range(self.n16)]
            self.i32 = 0
            self.i16 = 0
            self.psb = [stack.enter_context(nc.psum_tensor(f"ps{i}", [P, T], F32)) for i in range(8)]
            self.ips = 0

            S.dry = True
            self.wreq = []
            self.emit_all()
            S.dry = False
            self.worder = list(self.wreq)
            self.wreq = []
            self.wnext_dma = 0
            self.i32 = self.i16 = self.ips = 0
            self.emit_all()
            S.wait_all_dma("sync", "out")

            names = S.sem_names()
            sems = {n: stack.enter_context(nc.semaphore(n.replace(":", "_"))) for n in names}
            with nc.Block() as block:
                @block.tensor
                def _(e):
                    S.replay("tensor", e, sems)

                @block.vector
                def _(e):
                    S.replay("vector", e, sems)

                @block.scalar
                def _(e):
                    S.replay("scalar", e, sems)

                @block.gpsimd
                def _(e):
                    S.replay("gpsimd", e, sems)

                @block.sync
                def _(e):
                    S.replay("sync", e, sems)
        return nc

    def tmp32(self):
        i = self.i32
        self.i32 = (i + 1) % self.n32
        return self.t32[i], ("t32", i)

    def tmp16(self):
        i = self.i16
        self.i16 = (i + 1) % self.n16
        return self.t16[i], ("t16", i)

    def psum(self):
        i = self.ips
        self.ips = (i + 1) % 8
        return self.psb[i], ("ps", i)

    def pcol(self, name, i, n=1):
        o, k = self.poff[name]
        return self.pvec[:, o + i:o + i + n]

    def wget(self, part, l, gi):
        req = (part, l, gi)
        if self.S.dry:
            self.wreq.append(req)
            return self.wslot[0], ("w", 0)
        idx = len(self.wreq)
        assert self.worder[idx] == req
        self.wreq.append(req)
        while self.wnext_dma < len(self.worder) and self.wnext_dma <= idx + NSLOT - 1:
            self._wdma(self.wnext_dma)
            self.wnext_dma += 1
        s = idx % NSLOT
        return self.wslot[s], ("w", s)

    def wunit(self, part, l, ui):
        kind = "ffn" if part == "f" else LAYER_KIND[l]
        gi, coff = group_plan(unit_cols(kind))[1][ui] if kind not in self._plans else self._plans[kind][ui]
        cur = getattr(self, "_wcur", None)
        if cur is None or cur[0] != (part, l, gi, self._wtile):
            slot, slotk = self.wget(part, l, gi)
            self._wcur = ((part, l, gi, self._wtile), slot, slotk)
        _, slot, slotk = self._wcur
        return slot, slotk, coff // P, coff

    def _wdma(self, i):
        part, l, gi = self.worder[i]
        kind = "ffn" if part == "f" else LAYER_KIND[l]
        sizes = group_sizes(kind)
        off = sum(sizes[:gi]) * P
        n = sizes[gi]
        s = i % NSLOT
        dst = self.wslot[s][:, 0:n]
        scr = self.wb_d[(part, l)][off:off + n * P].rearrange("(p n) -> p n", p=P)
        gkey = ("wbg", part, l, gi)
        if (part, l, gi) not in self.wseen:
            self.wseen.add((part, l, gi))
            src = self.w_d[(part, l)][off:off + n * P].rearrange("(p n) -> p n", p=P)
            self.S.dma("gpsimd", lambda e, dst=dst, src=src: e.dma_start(out=dst, in_=src),
                       f"wsw{s}", reads=(), writes=[("w", s)])
            if self.NT > 1:
                self.S.dma("sync", lambda e, dst=dst, scr=scr: e.dma_start(out=scr, in_=dst),
                           f"wst{s}", reads=[("w", s)], writes=[gkey])
        else:
            self.S.dma("sync", lambda e, dst=dst, scr=scr: e.dma_start(out=dst, in_=scr),
                       f"w{s}", reads=[gkey], writes=[("w", s)])

    def emit_cast(self, part, l):
        if (part, l) in self.cast_done or (part, l) not in self.w_d:
            return
        self.cast_done.add((part, l))
        src = self.w_d[(part, l)]
        dst = self.wb_d[(part, l)]
        n = src.shape[0]
        CH = P * 8192
        off = 0
        while off < n:
            m = min(CH, n - off)
            a = src[off:off + m].rearrange("(p n) -> p n", p=P)
            b = dst[off:off + m].rearrange("(p n) -> p n", p=P)
            self.S.dma("gpsimd", lambda e, a=a, b=b: e.dma_start(out=b, in_=a), f"cast{part}{l}",
                       writes=[("wb", part, l)])
            off += m

    def emit_all(self):
        S = self.S
        if not S.dry:
            self.emit_setup()
        parts = []
        for (l, do_mix, do_ffn) in self.layers:
            if do_mix:
                parts.append(("m", l))
            if do_ffn:
                parts.append(("f", l))
        if not S.dry:
            self.wseen = set()
        self._wcur = None
        for ti in range(self.NT):
            self._wtile = ti
            t0 = ti * T
            xkeys = [("x", c) for c in range(KC)]
            src = self.x_d.rearrange("(c p) t -> p c t", p=P)[:, :, t0:t0 + T]
            S.dma("sync", lambda e, src=src: e.dma_start(out=self.xres[:, :, :], in_=src), "xin",
                  writes=xkeys)
            for (l, do_mix, do_ffn) in self.layers:
                kind = LAYER_KIND[l]
                if do_mix:
                    self.rmsnorm(self.gcol(f"mixnorm{l}"))
                    if kind == "ssd":
                        self.ssd_mixer(l, ti)
                    else:
                        self.sc_mixer(l, ti)
                if do_ffn:
                    self.rmsnorm(self.gcol(f"ffnnorm{l}"))
                    self.ffn(l, ti)
            if self.final_norm:
                self.rmsnorm(self.gcol("finalnorm"), inplace=True)
            dst = self.y_d.rearrange("(c p) t -> p c t", p=P)[:, :, t0:t0 + T]
            S.dma("sync", lambda e, dst=dst: e.dma_start(out=dst, in_=self.xres[:, :, :]), "out",
                  reads=xkeys)

    def gcol(self, name):
        names = []
        for l in range(DEPTH):
            names += [f"mixnorm{l}", f"ffnnorm{l}"]
        names.append("finalnorm")
        return names.index(name) * KC

    def emit_setup(self):
        S = self.S
        S.dma("sync", lambda e: e.dma_start(out=self.pvec[:, :], in_=self.pv_d[:, :]), "pv", writes=["pvec"])
        S.op("gpsimd", lambda e: e.memset(self.neghalf[:, :], -0.5), writes=["neghalf"])
        S.op("gpsimd", lambda e: e.memset(self.ones_bf[:, :], 1.0), writes=["ones"])
        S.op("gpsimd", lambda e: e.memset(self.carry_ffn[:, :, :, :], 0.0), writes=["carry_ffn"])
        S.op("gpsimd", lambda e: e.memset(self.carry_sc[:, :, :, :], 0.0), writes=["carry_sc"])
        names = []
        for l in range(DEPTH):
            names += [f"mixnorm{l}", f"ffnnorm{l}"]
        names.append("finalnorm")
        for i, nme in enumerate(names):
            o, k = self.poff[nme]
            S.op("vector", lambda e, i=i, o=o: e.tensor_scalar(
                out=self.gam[:, i * KC:(i + 1) * KC], in0=self.pvec[:, o:o + KC],
                scalar1=float(np.sqrt(D)), scalar2=None, op0=ALU.mult),
                reads=["pvec"], writes=["gam"])
        self.setup_ssd()

    def setup_ssd(self):
        if not self.has_ssd:
            return
        S = self.S
        g = "gpsimd"
        S.op(g, lambda e: e.memset(self.states[:, :, :, :], 0.0), writes=[("state", j, gg) for j in range(2) for gg in range(NG)])
        S.op(g, lambda e: e.memset(self.carry_ssd[:, :, :, :], 0.0), writes=["carry_ssd"])
        S.op(g, lambda e: e.memset(self.mle_f[:, :], 1.0), writes=["mle_f"])
        S.op(g, lambda e: e.affine_select(out=self.mle_f[:, :], in_=self.mle_f[:, :], pattern=[[1, P]],
                                          compare_op=ALU.is_ge, fill=0.0, base=0, channel_multiplier=-1),
             reads=["mle_f"], writes=["mle_f"])
        S.op(g, lambda e: e.tensor_copy(out=self.mle_bf[:, :], in_=self.mle_f[:, :]), reads=["mle_f"], writes=["mle_bf"])
        S.op(g, lambda e: e.tensor_scalar(out=self.mgt_bf[:, :], in0=self.mle_f[:, :], scalar1=-1.0, scalar2=1.0,
                                          op0=ALU.mult, op1=ALU.add), reads=["mle_f"], writes=["mgt_bf"])
        S.op(g, lambda e: e.memset(self.ident_bf[:, :], 1.0), writes=["ident"])
        S.op(g, lambda e: e.affine_select(out=self.ident_bf[:, :], in_=self.ident_bf[:, :], pattern=[[-1, P]],
                                          compare_op=ALU.is_equal, fill=0.0, base=0, channel_multiplier=1),
             reads=["ident"], writes=["ident"])
        for j in range(2):
            o, _ = self.poff[f"alog{j}"]
            S.op("scalar", lambda e, j=j, o=o: e.activation(out=self.aneg[:, j, :], in_=self.pvec[:, o:o + NH], func=AF.Exp),
                 reads=["pvec"], writes=[("aneg", j)])
            S.op("vector", lambda e, j=j: e.tensor_scalar(out=self.aneg[:, j, :], in0=self.aneg[:, j, :], scalar1=-1.0,
                                                          scalar2=None, op0=ALU.mult),
                 reads=[("aneg", j)], writes=[("aneg", j)])
            o, _ = self.poff[f"ssdnormw{j}"]
            S.op("vector", lambda e, j=j, o=o: e.tensor_scalar(out=self.gnw[:, j, :], in0=self.pvec[:, o:o + 32],
                                                               scalar1=float(np.sqrt(512.0)), scalar2=None, op0=ALU.mult),
                 reads=["pvec"], writes=["gnw"])

    def rmsnorm(self, gcol0, inplace=False):
        S = self.S
        ps, psk = self.psum()
        for c in range(KC):
            sq, sqk = self.tmp16()
            S.op("scalar", lambda e, c=c, sq=sq: e.activation(out=sq[:, 0:T], in_=self.xres[:, c, :], func=AF.Square),
                 reads=[("x", c)], writes=[sqk])
            S.op("tensor", lambda e, c=c, sq=sq: e.matmul(ps[:, :], self.ones_bf[:, :], sq[:, 0:T],
                                                          start=(c == 0), stop=(c == KC - 1)),
                 reads=[sqk, "ones"], writes=[psk], inc=True)
        t, tk = self.tmp32()
        S.op("scalar", lambda e: e.activation(out=t[:, 0:T], in_=ps[:, :], func=AF.Ln, bias=float(D * EPS)),
             reads=[psk], writes=[tk])
        S.op("scalar", lambda e: e.activation(out=self.rstd[:, :], in_=t[:, 0:T], func=AF.Exp, scale=-0.5),
             reads=[tk], writes=["rstd"])
        for c in range(KC):
            if inplace:
                out = self.xres[:, c, :]
                wk = ("x", c)
            else:
                out = self.h[:, c, :]
                wk = ("h", c)
            S.op("vector", lambda e, c=c, out=out: e.scalar_tensor_tensor(
                out=out, in0=self.xres[:, c, :], scalar=self.gam[:, gcol0 + c:gcol0 + c + 1],
                in1=self.rstd[:, :], op0=ALU.mult, op1=ALU.mult),
                reads=[("x", c), "rstd", "gam"], writes=[wk])

    def mm_fm(self, slot, slotk, blk0, nk, rhs_fn, rhs_keys, ps, psk, k0=0, ktot=None):
        S = self.S
        if ktot is None:
            ktot = nk
        for k in range(nk):
            kg = k0 + k
            S.op("tensor", lambda e, k=k, kg=kg: e.matmul(ps[:, :], slot[:, (blk0 + k) * P:(blk0 + k + 1) * P], rhs_fn(kg),
                                                          start=(kg == 0), stop=(kg == ktot - 1)),
                 reads=[slotk] + list(rhs_keys), writes=[psk], inc=(k == nk - 1))

    def conv_chain(self, specs):
        S = self.S
        maxw = max(s["width"] for s in specs)
        for step in range(maxw):
            for s in specs:
                wd = s["width"]
                if step >= wd:
                    continue
                if step == 0 and s.get("first_done"):
                    continue
                tap = wd - 1 - step
                shift = step
                src = s["ub"][:, 4 - shift:4 - shift + T]
                acc = s["acc"]
                if step == 0:
                    if s["b"] is not None:
                        S.op("vector", lambda e, src=src, acc=acc, s=s, tap=tap: e.tensor_scalar(
                            out=acc[:, 0:T], in0=src, scalar1=s["w"][tap], scalar2=s["b"], op0=ALU.mult, op1=ALU.add),
                            reads=[s["ubk"], (s["ubk"], "h"), "pvec"], writes=[s["acck"]])
                    else:
                        S.op("vector", lambda e, src=src, acc=acc, s=s, tap=tap: e.tensor_scalar(
                            out=acc[:, 0:T], in0=src, scalar1=s["w"][tap], scalar2=None, op0=ALU.mult),
                            reads=[s["ubk"], (s["ubk"], "h"), "pvec"], writes=[s["acck"]])
                else:
                    S.op("vector", lambda e, src=src, acc=acc, s=s, tap=tap: e.scalar_tensor_tensor(
                        out=acc[:, 0:T], in0=src, scalar=s["w"][tap], in1=acc[:, 0:T], op0=ALU.mult, op1=ALU.add),
                        reads=[s["ubk"], (s["ubk"], "h"), s["acck"], "pvec"], writes=[s["acck"]])

    def carry_io(self, ub, ubk, carry_ap, carryk, width):
        S = self.S
        hw = width - 1
        S.op("gpsimd", lambda e: e.tensor_copy(out=ub[:, 4 - hw:4], in_=carry_ap), reads=[carryk],
             writes=[ubk, (ubk, "h")])

    def carry_save(self, ub, ubk, carry_ap, carryk, width):
        S = self.S
        hw = width - 1
        S.op("gpsimd", lambda e: e.tensor_copy(out=carry_ap, in_=ub[:, 4 + T - hw:4 + T]), reads=[ubk, (ubk, "h")],
             writes=[carryk])

    def ffn(self, l, ti):
        S = self.S
        hkeys = [("h", k) for k in range(KC)]
        o_cv, _ = self.poff[f"ffnconv{l}"]
        for jp in range(FJ):
            for jj in range(1):
                j = jp
                specs = []
                for which in range(2):
                    ch = which * FJ + j
                    ps, psk = self.psum()
                    slot, slotk, b0, _ = self.wunit("f", l, 2 * j + which)
                    self.mm_fm(slot, slotk, b0, KC, lambda k: self.h[:, k, :], hkeys, ps, psk)
                    ub, ubk = self.tmp32()
                    acc, acck = self.tmp32()
                    car = self.carry_ffn[:, l, ch, :]
                    cark = ("carry_ffn", l, ch)
                    if ti == 0 and ("carry_ffn", l, ch) not in S.lastw and not S.dry:
                        S.lastw[cark] = S.lastw.get("carry_ffn")
                    self.carry_io(ub, ubk, car, cark, 3)
                    S.op("scalar", lambda e, ub=ub, ps=ps: e.activation(out=ub[:, 4:4 + T], in_=ps[:, :], func=AF.Copy),
                         reads=[psk], writes=[ubk])
                    self.carry_save(ub, ubk, car, cark, 3)
                    base = o_cv + ch * 4
                    S.op("scalar", lambda e, acc=acc, ps=ps, base=base: e.activation(
                        out=acc[:, 0:T], in_=ps[:, :], func=AF.Identity,
                        scale=self.pvec[:, base + 2:base + 3], bias=self.pvec[:, base + 3:base + 4]),
                        reads=[psk, "pvec"], writes=[acck])
                    specs.append(dict(ub=ub, ubk=ubk, acc=acc, acck=acck,
                                      w=[self.pvec[:, base + i:base + i + 1] for i in range(3)],
                                      b=self.pvec[:, base + 3:base + 4], width=3, first_done=True))
                self.conv_chain([dict(s, ubk=s["ubk"]) for s in specs])
                sg, sgk = self.tmp32()
                S.op("scalar", lambda e, sg=sg, a=specs[0]["acc"]: e.activation(out=sg[:, 0:T], in_=a[:, 0:T], func=AF.Silu),
                     reads=[specs[0]["acck"]], writes=[sgk])
                S.op("gpsimd", lambda e, sg=sg, a=specs[1]["acc"], j=j: e.tensor_tensor(
                    out=self.big[:, j, :], in0=sg[:, 0:T], in1=a[:, 0:T], op=ALU.mult),
                    reads=[sgk, specs[1]["acck"]], writes=[("big", j)])
        bkeys = [("big", j) for j in range(FJ)]
        for m in range(KC):
            ps, psk = self.psum()
            for hf in range(2):
                slot, slotk, b0, _ = self.wunit("f", l, 2 * FJ + 2 * m + hf)
                self.mm_fm(slot, slotk, b0, FJ // 2, lambda k: self.big[:, k, :], bkeys, ps, psk, k0=hf * (FJ // 2), ktot=FJ)
            S.op("vector", lambda e, m=m, ps=ps: e.tensor_tensor(out=self.xres[:, m, :], in0=self.xres[:, m, :],
                                                                 in1=ps[:, :], op=ALU.add),
                 reads=[("x", m), psk], writes=[("x", m)])

    def sc_mixer(self, l, ti):
        S = self.S
        jl = l // 2
        hkeys = [("h", k) for k in range(KC)]
        o_cv, _ = self.poff[f"scconv{jl}"]
        for j in range(KC):
            pss = []
            for which in range(3):
                ps, psk = self.psum()
                slot, slotk, b0, _ = self.wunit("m", l, 3 * j + which)
                self.mm_fm(slot, slotk, b0, KC, lambda k: self.h[:, k, :], hkeys, ps, psk)
                pss.append((ps, psk))
            cgs, cgk = self.tmp32()
            S.op("scalar", lambda e, cgs=cgs, ps=pss[1][0]: e.activation(out=cgs[:, 0:T], in_=ps[:, :], func=AF.Copy),
                 reads=[pss[1][1]], writes=[cgk])
            ub, ubk = self.tmp32()
            acc, acck = self.tmp32()
            car = self.carry_sc[:, jl, j, :]
            cark = ("carry_sc", jl, j)
            if ti == 0 and cark not in S.lastw and not S.dry:
                S.lastw[cark] = S.lastw.get("carry_sc")
            self.carry_io(ub, ubk, car, cark, 3)
            S.op("vector", lambda e, ub=ub, cgs=cgs, ps=pss[2][0]: e.tensor_tensor(
                out=ub[:, 4:4 + T], in0=cgs[:, 0:T], in1=ps[:, :], op=ALU.mult),
                reads=[cgk, pss[2][1]], writes=[ubk])
            self.carry_save(ub, ubk, car, cark, 3)
            base = o_cv + j * 3
            self.conv_chain([dict(ub=ub, ubk=ubk, acc=acc, acck=acck,
                                  w=[self.pvec[:, base + i:base + i + 1] for i in range(3)], b=None, width=3)])
            S.op("vector", lambda e, acc=acc, ps=pss[0][0], j=j: e.tensor_tensor(
                out=self.big[:, j, :], in0=acc[:, 0:T], in1=ps[:, :], op=ALU.mult),
                reads=[acck, pss[0][1]], writes=[("big", j)])
        bkeys = [("big", j) for j in range(KC)]
        for mg in range(8):
            for mi in range(2):
                m = mg * 2 + mi
                ps, psk = self.psum()
                slot, slotk, b0, _ = self.wunit("m", l, 3 * KC + m)
                self.mm_fm(slot, slotk, b0, KC, lambda k: self.big[:, k, :], bkeys, ps, psk)
                S.op("vector", lambda e, m=m, ps=ps: e.tensor_tensor(out=self.xres[:, m, :], in0=self.xres[:, m, :],
                                                                     in1=ps[:, :], op=ALU.add),
                     reads=[("x", m), psk], writes=[("x", m)])

    def ssd_mixer(self, l, ti):
        S = self.S
        jl = l // 2
        hkeys = [("h", k) for k in range(KC)]
        o_cv, _ = self.poff[f"ssdconv{jl}"]
        o_db, _ = self.poff[f"dtbias{jl}"]
        o_d, _ = self.poff[f"ssdd{jl}"]
        big = self.big
        bc = lambda ap, shape: ap.to_broadcast(shape)
        slot, slotk, _, co = self.wunit("m", l, 0)
        ps, psk = self.psum()
        for c in range(4):
            for k in range(KC):
                S.op("tensor", lambda e, k=k, c=c, ps=ps, slot=slot, co=co: e.matmul(
                    ps[:, c * NH:(c + 1) * NH], self.h[:, k, c * P:(c + 1) * P], slot[:, co + k * NH:co + (k + 1) * NH],
                    start=(k == 0), stop=(k == KC - 1)), reads=[slotk] + hkeys, writes=[psk], inc=(k == KC - 1))
        t1, t1k = self.tmp32()
        v3 = lambda ap: ap.rearrange("p (c h) -> p c h", c=4)
        S.op("vector", lambda e, ps=ps, t1=t1: e.tensor_tensor(
            out=v3(t1[:, 0:4 * NH]), in0=v3(ps[:, 0:4 * NH]),
            in1=bc(self.pvec[:, o_db:o_db + NH].unsqueeze(1), [P, 4, NH]), op=ALU.add),
            reads=[psk, "pvec"], writes=[t1k])
        S.op("scalar", lambda e, t1=t1: e.activation(out=t1[:, 0:4 * NH], in_=t1[:, 0:4 * NH], func=AF.Exp),
             reads=[t1k], writes=[t1k])
        S.op("scalar", lambda e, t1=t1: e.activation(out=self.dtt[:, :, :].rearrange("p c h -> p (c h)"), in_=t1[:, 0:4 * NH],
                                                     func=AF.Ln, bias=1.0),
             reads=[t1k], writes=[("dtt", c) for c in range(4)])
        S.op("vector", lambda e: e.tensor_tensor(out=self.dat[:, :, :], in0=self.dtt[:, :, :],
                                                 in1=bc(self.aneg[:, jl, :].unsqueeze(1), [P, 4, NH]), op=ALU.mult),
             reads=[("dtt", c) for c in range(4)] + [("aneg", jl)], writes=[("dat", c) for c in range(4)])
        S.op("vector", lambda e: e.tensor_copy(out=self.da_bf[:, :, :], in_=self.dat[:, :, :]),
             reads=[("dat", c) for c in range(4)], writes=[("da_bf", c) for c in range(4)])
        for (lhs, lhsk, dst, dstk) in ((self.mle_bf, "mle_bf", self.expcs, "expcs"),
                                       (self.mgt_bf, "mgt_bf", self.dte, "dte"),
                                       (self.ones_bf, "ones", self.cdb, "cdb")):
            ps2, ps2k = self.psum()
            for c in range(4):
                S.op("tensor", lambda e, ps2=ps2, lhs=lhs, c=c: e.matmul(ps2[:, c * NH:(c + 1) * NH], lhs[:, :], self.da_bf[:, c, :],
                                                                         start=True, stop=True),
                     reads=[lhsk] + [("da_bf", cc) for cc in range(4)], writes=[ps2k], inc=(c == 3))
            S.op("scalar", lambda e, ps2=ps2, dst=dst: e.activation(out=dst[:, :, :].rearrange("p c h -> p (c h)"),
                                                                    in_=ps2[:, 0:4 * NH], func=AF.Exp),
                 reads=[ps2k], writes=[(dstk, c) for c in range(4)])
        ZS, XS, BG, CG = 32, 36, 40, 41
        for g in range(NG):
            stk = ("state", jl, g)
            S.op("scalar", lambda e, g=g: e.activation(out=self.state_bf[:, :], in_=self.states[:, jl, g, :], func=AF.Copy),
                 reads=[stk], writes=["state_bf"])
            for r in range(4):
                S.op("vector", lambda e, r=r, g=g: e.tensor_scalar(
                    out=self.diag4[:, r * P:(r + 1) * P], in0=self.ident_bf[:, :],
                    scalar1=self.pvec[:, o_d + 4 * g + r:o_d + 4 * g + r + 1], scalar2=None, op0=ALU.mult),
                    reads=["ident", "pvec"], writes=["diag4"])
            for half in range(2):
                for rr in range(3):
                    r = half * 3 + rr
                    ci = 6 * g + r
                    ps, psk = self.psum()
                    slot, slotk, b0, _ = self.wunit("m", l, 1 + 10 * g + r)
                    self.mm_fm(slot, slotk, b0, KC, lambda k: self.h[:, k, :], hkeys, ps, psk)
                    ub, ubk = self.tmp32()
                    acc, acck = self.tmp32()
                    car = self.carry_ssd[:, jl, ci, :]
                    cark = ("carry_ssd", jl, ci)
                    if ti == 0 and cark not in S.lastw and not S.dry:
                        S.lastw[cark] = S.lastw.get("carry_ssd")
                    self.carry_io(ub, ubk, car, cark, 4)
                    S.op("scalar", lambda e, ub=ub, ps=ps: e.activation(out=ub[:, 4:4 + T], in_=ps[:, :], func=AF.Copy),
                         reads=[psk], writes=[ubk])
                    self.carry_save(ub, ubk, car, cark, 4)
                    base = o_cv + ci * 5
                    S.op("scalar", lambda e, acc=acc, ps=ps, base=base: e.activation(
                        out=acc[:, 0:T], in_=ps[:, :], func=AF.Identity,
                        scale=self.pvec[:, base + 3:base + 4], bias=self.pvec[:, base + 4:base + 5]),
                        reads=[psk, "pvec"], writes=[acck])
                    self.conv_chain([dict(ub=ub, ubk=ubk, acc=acc, acck=acck,
                                          w=[self.pvec[:, base + i:base + i + 1] for i in range(4)],
                                          b=self.pvec[:, base + 4:base + 5], width=4, first_done=True)])
                    th, thk = self.tmp32()
                    S.op("scalar", lambda e, th=th, acc=acc: e.activation(out=th[:, 0:T], in_=acc[:, 0:T], func=AF.Tanh, scale=0.5),
                         reads=[acck], writes=[thk])
                    S.op("gpsimd", lambda e, th=th: e.tensor_scalar(out=th[:, 0:T], in0=th[:, 0:T], scalar1=0.5, scalar2=0.5,
                                                                    op0=ALU.mult, op1=ALU.add), reads=[thk], writes=[thk])
                    dsti = XS + r if r < 4 else (BG if r == 4 else CG)
                    S.op("gpsimd", lambda e, th=th, acc=acc, dsti=dsti: e.tensor_tensor(
                        out=big[:, dsti, :], in0=th[:, 0:T], in1=acc[:, 0:T], op=ALU.mult),
                        reads=[thk, acck], writes=[("big", dsti)])
            zb = [self.psum() for _ in range(4)]
            for half in range(4):
                slot, slotk, _, co = self.wunit("m", l, 1 + 10 * g + 6 + half)
                for c in range(4):
                    for kk in range(4):
                        k = half * 4 + kk
                        S.op("tensor", lambda e, c=c, k=k, kk=kk, slot=slot, zb=zb, co=co: e.matmul(
                            zb[c][0][:, :], self.h[:, k, c * P:(c + 1) * P], slot[:, co + kk * T:co + (kk + 1) * T],
                            start=(k == 0), stop=(k == KC - 1)),
                            reads=[slotk] + hkeys, writes=[zb[c][1]], inc=(kk == 3))
            for c in range(4):
                th, thk = self.tmp32()
                S.op("scalar", lambda e, th=th, c=c, zb=zb: e.activation(out=th[:, 0:T], in_=zb[c][0][:, :], func=AF.Tanh, scale=0.5),
                     reads=[zb[c][1]], writes=[thk])
                S.op("gpsimd", lambda e, th=th: e.tensor_scalar(out=th[:, 0:T], in0=th[:, 0:T], scalar1=0.5, scalar2=0.5,
                                                                op0=ALU.mult, op1=ALU.add), reads=[thk], writes=[thk])
                S.op("vector", lambda e, th=th, c=c, zb=zb: e.tensor_tensor(out=big[:, ZS + c, :], in0=th[:, 0:T], in1=zb[c][0][:, :],
                                                                     op=ALU.mult),
                     reads=[thk, zb[c][1]], writes=[("big", ZS + c)])
            hs = slice(HPG * g, HPG * (g + 1))
            fb = {}

            def front(c, g=g, hs=hs, fb=fb):
                cs = slice(c * P, (c + 1) * P)
                par = c % 2
                ps, psk = self.psum()
                S.op("tensor", lambda e, ps=ps, cs=cs: e.matmul(ps[:, 0:P], big[:, BG, cs], big[:, CG, cs], start=True, stop=True),
                     reads=[("big", BG), ("big", CG)], writes=[psk])
                cbt, cbtk = self.s_cbt, "s_cbt"
                S.op("vector", lambda e, ps=ps, cbt=cbt: e.tensor_tensor(out=cbt[:, 0:P], in0=ps[:, 0:P], in1=self.mle_f[:, :],
                                                                          op=ALU.mult), reads=[psk, "mle_f"], writes=[cbtk])
                pst, pstk = self.psum()
                pstb = pst.bitcast(BF16)
                for r in range(4):
                    S.op("tensor", lambda e, r=r, cs=cs, pstb=pstb: e.transpose(pstb[:, r * P:(r + 1) * P], big[:, XS + r, cs],
                                                                                 self.ident_bf[:, :]),
                         reads=[("big", XS + r), "ident"], writes=[pstk], inc=False)
                S.op("tensor", lambda e, cs=cs, pstb=pstb: e.transpose(pstb[:, 4 * P:5 * P], big[:, BG, cs], self.ident_bf[:, :]),
                     reads=[("big", BG), "ident"], writes=[pstk])
                yield
                xdt, xdtk = (self.s_xdt, "s_xdt") if par == 0 else (self.s_xdt2, "s_xdt2")
                S.op("vector", lambda e, pstb=pstb, xdt=xdt, c=c, hs=hs: e.tensor_tensor(
                    out=xdt[:, 0:T].rearrange("p (h d) -> p h d", h=HPG),
                    in0=pstb[:, 0:T].rearrange("p (h d) -> p h d", h=HPG),
                    in1=bc(self.dtt[:, c, hs].unsqueeze(2), [P, HPG, HD]), op=ALU.mult),
                    reads=[pstk, ("dtt", c)], writes=[xdtk])
                btm, btmk = (self.s_btm, "s_btm") if par == 0 else (self.s_btm2, "s_btm2")
                S.op("scalar", lambda e, pstb=pstb, btm=btm: e.activation(out=btm[:, 0:P], in_=pstb[:, 4 * P:5 * P], func=AF.Copy),
                     reads=[pstk], writes=[btmk])
                yield
                xdtd, xdtdk = (self.s_xdtd, "s_xdtd") if par == 0 else (self.s_xdtd2, "s_xdtd2")
                S.op("gpsimd", lambda e, xdt=xdt, xdtd=xdtd, c=c, hs=hs: e.tensor_tensor(
                    out=xdtd[:, 0:T].rearrange("p (h d) -> p h d", h=HPG),
                    in0=xdt[:, 0:T].rearrange("p (h d) -> p h d", h=HPG),
                    in1=bc(self.dte[:, c, hs].unsqueeze(2), [P, HPG, HD]), op=ALU.mult),
                    reads=[xdtk, ("dte", c)], writes=[xdtdk])
                rd, rdk = self.s_rd, "s_rd"
                S.op("vector", lambda e, rd=rd, c=c, hs=hs: e.tensor_tensor(
                    out=rd[:, :].rearrange("p (h l) -> p h l", h=HPG),
                    in0=bc(self.mle_bf[:, :].unsqueeze(1), [P, HPG, P]),
                    in1=bc(self.dat[:, c, hs].unsqueeze(2), [P, HPG, P]), op=ALU.mult),
                    reads=["mle_bf", ("dat", c)], writes=[rdk])
                yield
                ee, eek = self.s_ee, "s_ee"
                for hh2 in range(2):
                    psd, psdk = self.psum()
                    S.op("tensor", lambda e, psd=psd, rd=rd, hh2=hh2: e.matmul(psd[:, :], self.mgt_bf[:, :],
                                                                               rd[:, hh2 * T:(hh2 + 1) * T], start=True, stop=True),
                         reads=["mgt_bf", rdk], writes=[psdk])
                    S.op("scalar", lambda e, psd=psd, ee=ee, hh2=hh2: e.activation(out=ee[:, hh2 * T:(hh2 + 1) * T], in_=psd[:, :],
                                                                                   func=AF.Exp),
                         reads=[psdk], writes=[(eek, hh2)])
                yield
                mt, mtk = (self.s_mt, "s_mt") if par == 0 else (self.s_mt2, "s_mt2")
                S.op("vector", lambda e, mt=mt, ee=ee, cbt=cbt: e.tensor_tensor(
                    out=mt[:, :].rearrange("p (h l) -> p h l", h=HPG),
                    in0=ee[:, :].rearrange("p (h l) -> p h l", h=HPG),
                    in1=bc(cbt[:, 0:P].unsqueeze(1), [P, HPG, P]), op=ALU.mult),
                    reads=[(eek, 0), (eek, 1), eek, cbtk], writes=[mtk])
                fb[c] = (xdt, xdtk, btm, btmk, xdtd, xdtdk, mt, mtk)

            def back(c, g=g, hs=hs, fb=fb, stk=stk):
                cs = slice(c * P, (c + 1) * P)
                xdt, xdtk, btm, btmk, xdtd, xdtdk, mt, mtk = fb[c]
                psy, psyk = self.psum()
                for r in range(4):
                    S.op("tensor", lambda e, r=r, cs=cs, psy=psy: e.matmul(
                        psy[:, r * P:(r + 1) * P], big[:, XS + r, cs], self.diag4[:, r * P:(r + 1) * P],
                        start=(r == 0), stop=False, skip_group_check=True),
                        reads=[("big", XS + r), "diag4"], writes=[psyk], inc=False)
                for hh in range(HPG):
                    S.op("tensor", lambda e, hh=hh, psy=psy, mt=mt, xdt=xdt: e.matmul(
                        psy[:, hh * HD:(hh + 1) * HD], mt[:, hh * P:(hh + 1) * P], xdt[:, hh * HD:(hh + 1) * HD],
                        start=False, stop=(hh == HPG - 1), skip_group_check=True),
                        reads=[mtk, xdtk], writes=[psyk], inc=(hh == HPG - 1))
                pso, psok = self.psum()
                S.op("tensor", lambda e, pso=pso, cs=cs: e.matmul(pso[:, :], big[:, CG, cs], self.state_bf[:, :], start=True, stop=True),
                     reads=[("big", CG), "state_bf"], writes=[psok])
                yield
                yt, ytk = self.tmp32()
                S.op("vector", lambda e, pso=pso, yt=yt, c=c, hs=hs: e.tensor_tensor(
                    out=yt[:, 0:T].rearrange("p (h d) -> p h d", h=HPG),
                    in0=pso[:, :].rearrange("p (h d) -> p h d", h=HPG),
                    in1=bc(self.expcs[:, c, hs].unsqueeze(2), [P, HPG, HD]), op=ALU.mult),
                    reads=[psok, ("expcs", c)], writes=[ytk])
                S.op("vector", lambda e, psy=psy, yt=yt: e.tensor_tensor(out=yt[:, 0:T], in0=yt[:, 0:T], in1=psy[:, :], op=ALU.add),
                     reads=[psyk, ytk], writes=[ytk])
                yield
                S.op("gpsimd", lambda e, yt=yt, c=c: e.tensor_tensor(out=yt[:, 0:T], in0=yt[:, 0:T], in1=big[:, ZS + c, :], op=ALU.mult),
                     reads=[ytk, ("big", ZS + c)], writes=[ytk])
                yield
                junk, junkk = self.tmp32()
                col = (g * 4 + c) % 8
                S.op("scalar", lambda e, yt=yt, junk=junk, col=col: e.activation(out=junk[:, 0:T], in_=yt[:, 0:T], func=AF.Square,
                                                                                 accum_out=self.ssq[:, col:col + 1]),
                     reads=[ytk], writes=[junkk, ("ssq", col)])
                yield
                S.op("vector", lambda e, col=col: e.tensor_scalar(out=self.ssq[:, col:col + 1], in0=self.ssq[:, col:col + 1],
                                                                  scalar1=float(512 * EPS), scalar2=None, op0=ALU.add),
                     reads=[("ssq", col)], writes=[("ssq", col)])
                yield
                S.op("gpsimd", lambda e, col=col: e.tensor_tensor(out=self.ssq[:, col:col + 1], in0=self.ssq[:, col:col + 1],
                                                                  in1=self.neghalf[:, 0:1], op=ALU.pow),
                     reads=[("ssq", col), "neghalf"], writes=[("ssq", col)])
                yield
                yn, ynk = self.s_yn, "s_yn"
                S.op("scalar", lambda e, yt=yt, yn=yn, col=col: e.activation(out=yn[:, 0:T], in_=yt[:, 0:T], func=AF.Copy,
                                                                             scale=self.ssq[:, col:col + 1]),
                     reads=[ytk, ("ssq", col)], writes=[ynk])
                yield
                pt3, pt3k = self.psum()
                pt3b = pt3.bitcast(BF16)
                for r in range(4):
                    S.op("tensor", lambda e, r=r, yn=yn, pt3b=pt3b: e.transpose(pt3b[:, r * P:(r + 1) * P], yn[:, r * P:(r + 1) * P],
                                                                                self.ident_bf[:, :]),
                         reads=[ynk, "ident"], writes=[pt3k], inc=(r == 3))
                yield
                for r in range(4):
                    S.op("scalar", lambda e, r=r, pt3b=pt3b, cs=cs, g=g: e.activation(
                        out=big[:, 4 * g + r, cs], in_=pt3b[:, r * P:(r + 1) * P], func=AF.Copy,
                        scale=self.gnw[:, jl, 4 * g + r:4 * g + r + 1]),
                        reads=[pt3k, "gnw"], writes=[("big", 4 * g + r)])

            def stateg(c, g=g, hs=hs, fb=fb, stk=stk):
                xdt, xdtk, btm, btmk, xdtd, xdtdk, mt, mtk = fb[c]
                pss, pssk = self.psum()
                S.op("tensor", lambda e, pss=pss, btm=btm, xdtd=xdtd: e.matmul(pss[:, :], btm[:, 0:P], xdtd[:, 0:T], start=True, stop=True),
                     reads=[btmk, xdtdk], writes=[pssk])
                yield
                S.op("gpsimd", lambda e, g=g, c=c, hs=hs: e.tensor_tensor(
                    out=self.states[:, jl, g, :].rearrange("p (h d) -> p h d", h=HPG),
                    in0=self.states[:, jl, g, :].rearrange("p (h d) -> p h d", h=HPG),
                    in1=bc(self.cdb[:, c, hs].unsqueeze(2), [P, HPG, HD]), op=ALU.mult),
                    reads=[stk, ("cdb", c)], writes=[stk])
                yield
                S.op("vector", lambda e, g=g, pss=pss: e.tensor_tensor(out=self.states[:, jl, g, :], in0=self.states[:, jl, g, :],
                                                                       in1=pss[:, :], op=ALU.add),
                     reads=[stk, pssk], writes=[stk])
                yield
                if c < 3:
                    S.op("scalar", lambda e, g=g: e.activation(out=self.state_bf[:, :], in_=self.states[:, jl, g, :], func=AF.Copy),
                         reads=[stk], writes=["state_bf"])
            for _ in front(0):
                pass
            for c in range(4):
                gens = [back(c), stateg(c)] + ([front(c + 1)] if c < 3 else [])
                while gens:
                    for gg in list(gens):
                        try:
                            next(gg)
                        except StopIteration:
                            gens.remove(gg)
        ykeys = [("big", j) for j in range(32)]
        for m in range(KC):
            ps, psk = self.psum()
            slot, slotk, b0, _ = self.wunit("m", l, 81 + m)
            self.mm_fm(slot, slotk, b0, 32, lambda k: big[:, k, :], ykeys, ps, psk)
            S.op("vector", lambda e, m=m, ps=ps: e.tensor_tensor(out=self.xres[:, m, :], in0=self.xres[:, m, :],
                                                                 in1=ps[:, :], op=ALU.add),
                 reads=[("x", m), psk], writes=[("x", m)])


def pack_layer_weights(inp, l, do_mix=True, do_ffn=True):
    kind = LAYER_KIND[l]
    j = l // 2
    wm = wf = None
    if do_mix:
        if kind == "ssd":
            gm = pack_ssd(inp["ssd_w_in"][j], inp["ssd_w_out"][j])
        else:
            gm = pack_sc(inp["sc_w_in"][j], inp["sc_w_out"][j])
        wm = flat_groups(gm, kind)
    if do_ffn:
        gf = pack_ffn(inp["ffn_w_up"][l], inp["ffn_w_down"][l])
        wf = flat_groups(gf, "ffn")
    return wm, wf


def run_layers(x_fm_list, inp_np, layers, final_norm, S_len):
    b = Builder(S_len, layers, final_norm)
    nc = b.build()
    pv = pack_pvec(inp_np)
    shared = {"pvec": pv}
    for (l, do_mix, do_ffn) in layers:
        wm, wf = pack_layer_weights(inp_np, l, do_mix, do_ffn)
        if do_mix:
            shared[f"wm{l}"] = wm
        if do_ffn:
            shared[f"wf{l}"] = wf
    in_maps = []
    for xf in x_fm_list:
        m = dict(shared)
        m["x"] = np.ascontiguousarray(xf)
        in_maps.append(m)
    res = run_bass_kernel_spmd(nc, in_maps, core_ids=list(range(len(x_fm_list))))
    return [r["y"] for r in res.results]


def kernel(**inputs):
    inp = {k: np.asarray(v) for k, v in inputs.items()}
    x = inp["x"]
    B, S_len, _ = x.shape
    xs = [np.ascontiguousarray(x[b].T) for b in range(B)]
    ys = run_layers(xs, inp, [(l, True, True) for l in range(DEPTH)], True, S_len)
    out = np.stack([np.ascontiguousarray(y.T) for y in ys], axis=0).astype(np.float32)
    return out
```
